# Optimizing a Trainium2 kernel written in Bass

```python
import jax, jax.numpy as jnp
from jax import lax
import numpy as np

D_MODEL = 2048
BATCH = 8
SEQ = 4096
DEPTH = 1
DEC_BATCH = 32
DEC_SEQ = 32
PAST_LEN = 1024

CHUNK = 64
N_HEADS = 16
N_KV_HEADS = 4
HEAD_DIM = 64
GROUP = N_HEADS // N_KV_HEADS
ATTN_DIM = N_HEADS * HEAD_DIM
KV_DIM = N_KV_HEADS * HEAD_DIM
WINDOW = 128
WIN_CHUNKS = WINDOW // CHUNK
CONV_DIM = 1024
CONV_WIDTH = 31
ROPE_THETA = 10000.0
RMS_EPS = 1e-6
LN_EPS = 1e-5
NEG_INF = -1e30

Q_END = ATTN_DIM
K_END = Q_END + KV_DIM
V_END = K_END + KV_DIM
GA_END = V_END + ATTN_DIM
CU_END = GA_END + 2 * CONV_DIM
GB_END = CU_END + CONV_DIM
IN_DIM = GB_END + 2 * D_MODEL
IN_SPLITS = (Q_END, K_END, V_END, GA_END, CU_END, GB_END)

kernel_name = "hybrid_swa_sink_conformer_conv_stream_step"


def _rmsnorm(x, g):
    x32 = x.astype(jnp.float32)
    y = x32 * lax.rsqrt(jnp.mean(x32 * x32, axis=-1, keepdims=True) + RMS_EPS)
    return (y * g.astype(jnp.float32)).astype(x.dtype)


def _rope(x, pos):
    half = HEAD_DIM // 2
    inv = ROPE_THETA ** (-2.0 * jnp.arange(half, dtype=jnp.float32) / HEAD_DIM)
    ang = pos.astype(jnp.float32)[:, None] * inv[None, :]
    cos = jnp.cos(ang)[None, :, None, :]
    sin = jnp.sin(ang)[None, :, None, :]
    x1 = x[..., :half].astype(jnp.float32)
    x2 = x[..., half:].astype(jnp.float32)
    out = jnp.concatenate([x1 * cos - x2 * sin, x2 * cos + x1 * sin], axis=-1)
    return out.astype(x.dtype)


def _in_proj(x, c, norm_g, w_ada, b_ada, w_in, pos):
    mod = jax.nn.silu(c) @ w_ada + b_ada
    shift, scale, gate = jnp.split(mod[:, None, :], 3, axis=-1)
    h = _rmsnorm(x, norm_g) * (1 + scale) + shift
    z = h @ w_in
    q, k, v, ga, cu, gb, mg = jnp.split(z, IN_SPLITS, axis=-1)
    b, t, _ = x.shape
    q = _rope(q.reshape(b, t, N_HEADS, HEAD_DIM), pos)
    k = _rope(k.reshape(b, t, N_KV_HEADS, HEAD_DIM), pos)
    v = v.reshape(b, t, N_KV_HEADS, HEAD_DIM)
    u = cu[..., :CONV_DIM] * jax.nn.sigmoid(cu[..., CONV_DIM:])
    return q, k, v, ga, u, gb, mg, gate


def _attend(q, k, v, mask, sinks):
    s = jnp.einsum('...qkgd,...skd->...kgqs', q, k,
                   preferred_element_type=jnp.float32) * (HEAD_DIM ** -0.5)
    if mask is not None:
        s = jnp.where(mask, s, NEG_INF)
    sink = sinks.astype(jnp.float32).reshape(N_KV_HEADS, GROUP)[:, :, None, None]
    m = jnp.maximum(jnp.max(s, axis=-1, keepdims=True), sink)
    e = jnp.exp(s - m)
    p = e / (jnp.sum(e, axis=-1, keepdims=True) + jnp.exp(sink - m))
    return jnp.einsum('...kgqs,...skd->...qkgd', p.astype(v.dtype), v)


def _band_attention(q, k, v, sinks):
    b, s = q.shape[0], q.shape[1]
    nc = s // CHUNK
    qc = q.reshape(b, nc, CHUNK, N_KV_HEADS, GROUP, HEAD_DIM)
    kc = k.reshape(b, nc, CHUNK, N_KV_HEADS, HEAD_DIM)
    vc = v.reshape(b, nc, CHUNK, N_KV_HEADS, HEAD_DIM)
    padw = ((0, 0), (WIN_CHUNKS, 0), (0, 0), (0, 0), (0, 0))
    kp = jnp.pad(kc, padw)
    vp = jnp.pad(vc, padw)
    kb = jnp.concatenate([kp[:, i:i + nc] for i in range(WIN_CHUNKS + 1)], axis=2)
    vb = jnp.concatenate([vp[:, i:i + nc] for i in range(WIN_CHUNKS + 1)], axis=2)
    src = jnp.arange(nc)[:, None] - WIN_CHUNKS + jnp.arange(WIN_CHUNKS + 1)[None, :]
    valid = jnp.repeat(src >= 0, CHUNK, axis=1)
    mask = valid[:, None, None, None, :]
    o = _attend(qc, kb, vb, mask, sinks)
    return o.reshape(b, s, ATTN_DIM)


def _conv_branch(u_ctx, w_dw, b_dw, ln_g, ln_b):
    y = lax.conv_general_dilated(u_ctx, w_dw[:, None, :], window_strides=(1,),
                                 padding='VALID',
                                 dimension_numbers=('NWC', 'WIO', 'NWC'),
                                 feature_group_count=CONV_DIM) + b_dw
    y32 = y.astype(jnp.float32)
    mu = jnp.mean(y32, axis=-1, keepdims=True)
    var = jnp.mean(jnp.square(y32 - mu), axis=-1, keepdims=True)
    yn = (y32 - mu) * lax.rsqrt(var + LN_EPS) * ln_g.astype(jnp.float32) + ln_b.astype(jnp.float32)
    return jax.nn.silu(yn).astype(u_ctx.dtype)


def _out_proj(x, attn_o, conv_o, ga, gb, mg, gate, w_proj_a, w_proj_b, w_out):
    pa = (attn_o * jax.nn.silu(ga)) @ w_proj_a
    pb = (conv_o * jax.nn.silu(gb)) @ w_proj_b
    merged = jax.nn.sigmoid(mg[..., :D_MODEL]) * pa + jax.nn.sigmoid(mg[..., D_MODEL:]) * pb
    return x + gate * (merged @ w_out)


def setup_inputs(seed: int = 0) -> dict:
    key = jax.random.key(seed)
    ks = jax.random.split(key, 20)
    f32 = jnp.float32
    win_cache = min(WINDOW, PAST_LEN)
    nrm = lambda k, shp, s: jax.random.normal(k, shp, f32) * s
    return {
        "x_prompt": nrm(ks[0], (BATCH, SEQ, D_MODEL), 1.0),
        "x_sample": nrm(ks[1], (DEC_BATCH, DEC_SEQ, D_MODEL), 1.0),
        "c_prompt": nrm(ks[2], (BATCH, D_MODEL), 1.0),
        "c_sample": nrm(ks[3], (DEC_BATCH, D_MODEL), 1.0),
        "cache_k": nrm(ks[4], (DEPTH, DEC_BATCH, win_cache, N_KV_HEADS, HEAD_DIM), 1.0),
        "cache_v": nrm(ks[5], (DEPTH, DEC_BATCH, win_cache, N_KV_HEADS, HEAD_DIM), 1.0),
        "state_conv": nrm(ks[6], (DEPTH, DEC_BATCH, CONV_WIDTH - 1, CONV_DIM), 0.5),
        "norm_g": 1.0 + nrm(ks[7], (DEPTH, D_MODEL), 0.02),
        "w_ada": nrm(ks[8], (DEPTH, D_MODEL, 3 * D_MODEL), 0.5 * D_MODEL ** -0.5),
        "b_ada": nrm(ks[9], (DEPTH, 3 * D_MODEL), 0.02),
        "w_in": nrm(ks[10], (DEPTH, D_MODEL, IN_DIM), D_MODEL ** -0.5),
        "sinks": nrm(ks[11], (DEPTH, N_HEADS), 1.0),
        "w_dw": nrm(ks[12], (DEPTH, CONV_WIDTH, CONV_DIM), CONV_WIDTH ** -0.5),
        "b_dw": nrm(ks[13], (DEPTH, CONV_DIM), 0.02),
        "ln_g": 1.0 + nrm(ks[14], (DEPTH, CONV_DIM), 0.02),
        "ln_b": nrm(ks[15], (DEPTH, CONV_DIM), 0.02),
        "w_proj_a": nrm(ks[16], (DEPTH, ATTN_DIM, D_MODEL), ATTN_DIM ** -0.5),
        "w_proj_b": nrm(ks[17], (DEPTH, CONV_DIM, D_MODEL), CONV_DIM ** -0.5),
        "w_out": nrm(ks[18], (DEPTH, D_MODEL, D_MODEL), D_MODEL ** -0.5),
        "final_g": 1.0 + nrm(ks[19], (D_MODEL,), 0.02),
    }


def reference(x_prompt, x_sample, c_prompt, c_sample, cache_k, cache_v, state_conv,
              norm_g, w_ada, b_ada, w_in, sinks, w_dw, b_dw, ln_g, ln_b,
              w_proj_a, w_proj_b, w_out, final_g):
    seq = x_prompt.shape[1]
    dec_seq = x_sample.shape[1]
    n_win = cache_k.shape[2]
    pos_p = jnp.arange(seq)
    pos_s = PAST_LEN + jnp.arange(dec_seq)
    xp, xs = x_prompt, x_sample
    kp_l, vp_l, cp_l, ks_l, vs_l, cs_l = [], [], [], [], [], []
    for l in range(DEPTH):
        q, k, v, ga, u, gb, mg, gate = _in_proj(xp, c_prompt, norm_g[l], w_ada[l], b_ada[l], w_in[l], pos_p)
        attn_o = _band_attention(q, k, v, sinks[l])
        u_ctx = jnp.pad(u, ((0, 0), (CONV_WIDTH - 1, 0), (0, 0)))
        conv_o = _conv_branch(u_ctx, w_dw[l], b_dw[l], ln_g[l], ln_b[l])
        xp = _out_proj(xp, attn_o, conv_o, ga, gb, mg, gate, w_proj_a[l], w_proj_b[l], w_out[l])
        kp_l.append(k[:, -n_win:])
        vp_l.append(v[:, -n_win:])
        cp_l.append(u[:, -(CONV_WIDTH - 1):])
        q, k, v, ga, u, gb, mg, gate = _in_proj(xs, c_sample, norm_g[l], w_ada[l], b_ada[l], w_in[l], pos_s)
        k_all = jnp.concatenate([cache_k[l], k], axis=1)
        v_all = jnp.concatenate([cache_v[l], v], axis=1)
        qs = q.reshape(q.shape[0], dec_seq, N_KV_HEADS, GROUP, HEAD_DIM)
        attn_o = _attend(qs, k_all, v_all, None, sinks[l]).reshape(q.shape[0], dec_seq, ATTN_DIM)
        u_ctx = jnp.concatenate([state_conv[l], u], axis=1)
        conv_o = _conv_branch(u_ctx, w_dw[l], b_dw[l], ln_g[l], ln_b[l])
        xs = _out_proj(xs, attn_o, conv_o, ga, gb, mg, gate, w_proj_a[l], w_proj_b[l], w_out[l])
        ks_l.append(k_all[:, -n_win:])
        vs_l.append(v_all[:, -n_win:])
        cs_l.append(u_ctx[:, -(CONV_WIDTH - 1):])
    y_prompt = _rmsnorm(xp, final_g)
    y_sample = _rmsnorm(xs, final_g)
    new_k_prompt = jnp.stack(kp_l)
    new_v_prompt = jnp.stack(vp_l)
    new_conv_prompt = jnp.stack(cp_l)
    new_k_sample = jnp.stack(ks_l)
    new_v_sample = jnp.stack(vs_l)
    new_conv_sample = jnp.stack(cs_l)
    return (y_prompt, y_sample, new_k_prompt, new_v_prompt, new_conv_prompt, new_k_sample, new_v_sample, new_conv_sample)
```

```python
import numpy as np
import ml_dtypes
import concourse.bass as bass
import concourse.mybir as mybir
from concourse.bass_utils import run_bass_kernel_spmd

F32 = mybir.dt.float32
BF16 = mybir.dt.bfloat16
AF = mybir.ActivationFunctionType
ALU = mybir.AluOpType

D = 2048
SEQ = 4096
NT = 512
NPT = SEQ // NT
NCH_IN = 76
PG = 512
N_DMA_SEMS = 44
RMS_EPS = 1e-6
LN_EPS = 1e-5
SCALE = 0.125
LNB0, LNB1 = 6, 7


class Op:
    __slots__ = ("eng", "fn", "deps", "is_dma", "sem", "val", "signals", "idx", "tag")


class Sched:
    def __init__(self):
        self.ops = []
        self.lastw = {}
        self.readers = {}
        self.out_dmas = []
        self.tag = None

    def add(self, eng, fn, reads=(), writes=(), dma=False, is_out=False):
        op = Op()
        op.eng = eng
        op.fn = fn
        op.is_dma = dma
        op.signals = dma
        op.sem = None
        op.val = 0
        op.idx = len(self.ops)
        op.tag = self.tag
        deps = {}
        for r in reads:
            w = self.lastw.get(r)
            if w is not None:
                deps[w.idx] = w
        for r in writes:
            w = self.lastw.get(r)
            if w is not None:
                deps[w.idx] = w
            for rd in self.readers.get(r, ()):
                deps[rd.idx] = rd
        for r in reads:
            self.readers.setdefault(r, []).append(op)
        for r in writes:
            self.lastw[r] = op
            self.readers[r] = []
        red = {}
        for d in deps.values():
            if d.is_dma:
                red[("dma", d.idx)] = d
            else:
                k = d.eng
                if k not in red or red[k].idx < d.idx:
                    red[k] = d
        op.deps = list(red.values())
        self.ops.append(op)
        if is_out:
            self.out_dmas.append(op)
        return op

    def finalize(self, nc, sems, dma_sems):
        pools = {"pool": dma_sems[:12], "sp": dma_sems[12:]}
        state = {q: {"cnt": [0] * len(p), "last": [None] * len(p), "k": 0} for q, p in pools.items()}
        for op in self.ops:
            if op.is_dma:
                p = pools[op.eng]
                stq = state[op.eng]
                i = stq["k"] % len(p)
                stq["k"] += 1
                if stq["last"][i] is not None:
                    op.deps.append(stq["last"][i])
                stq["last"][i] = op
                stq["cnt"][i] += 1
                op.sem = p[i]
                op.val = 16 * stq["cnt"][i]
        for op in self.ops:
            for d in op.deps:
                if (not d.is_dma) and d.eng == "pe" and op.eng == "pe" and not op.is_dma:
                    continue
                d.signals = True
        ecnt = {}
        for op in self.ops:
            if not op.is_dma:
                op.sem = sems[op.eng]
                if op.signals:
                    ecnt[op.eng] = ecnt.get(op.eng, 0) + 1
                    op.val = ecnt[op.eng]

    def emit_engine(self, eng, e):
        waited = {}
        for op in self.ops:
            if op.eng != eng:
                continue
            needs = {}
            for d in op.deps:
                if (not d.is_dma) and d.eng == "pe" and eng == "pe" and not op.is_dma:
                    continue
                key = id(d.sem)
                if key not in needs or needs[key][1] < d.val:
                    needs[key] = (d.sem, d.val)
            for key, (s, v) in needs.items():
                if waited.get(key, 0) < v:
                    e.wait_ge(s, v)
                    waited[key] = v
            ins = op.fn(e)
            if op.signals:
                ins.then_inc(op.sem, 16 if op.is_dma else 1)


class Buf:
    def __init__(self, ap, vbase, nbytes):
        self.ap = ap
        self.vbase = vbase
        self.nbytes = nbytes

    def res(self, lo=0, hi=None):
        if hi is None:
            hi = self.nbytes
        return [("pg", p) for p in range((self.vbase + lo) // PG, (self.vbase + hi - 1) // PG + 1)]

    def cres(self, i, cbytes, n=1):
        return self.res(i * cbytes, (i + n) * cbytes)


def _esz(dt):
    return 2 if dt == BF16 else 4


class _StopBuild(Exception):
    pass


def build_nc(stop=None):
    nc = bass.Bass("TRN2", target_bir_lowering=False)
    S = Sched()

    def ckpt(name):
        if stop is not None and name == stop:
            raise _StopBuild()

    def din(name, shape, dt=F32):
        return nc.dram_tensor(name, list(shape), dt, kind="ExternalInput").ap()

    def dout(name, shape):
        return nc.dram_tensor(name, list(shape), F32, kind="ExternalOutput").ap()

    def dscr(name, shape, dt):
        return nc.dram_tensor(name, list(shape), dt, kind="Internal").ap()

    xp = din("xp", [SEQ, D])
    xs = din("xs", [128, D])
    ctp = din("ctp", [128, 16, 128])
    cts = din("cts", [128, 16, 128])
    w_ada_t = din("w_ada_t", [12, 128, 16, 512])
    b_ada_bc = din("b_ada_bc", [128, 6144])
    g_bc = din("g_bc", [128, D])
    fg_bc = din("fg_bc", [128, D])
    w_in_t = din("w_in_t", [NCH_IN, 128, 16 * 128])
    w_v_t = din("w_v_t", [128, 16 * 256])
    w_pa_t = din("w_pa_t", [16, 128, 8 * 128])
    w_pb_t = din("w_pb_t", [16, 128, 8 * 128])
    w_out_t = din("w_out_t", [8, 128, 16 * 256])
    cos_p = din("cos_p", [128, SEQ])
    sin_p = din("sin_p", [128, SEQ])
    cos_s = din("cos_s", [128, 128])
    sin_s = din("sin_s", [128, 128])
    rmat_d = din("rmat", [128, 128], BF16)
    identb_d = din("identb", [128, 128], BF16)
    onesb_d = din("onesb", [128, 128], BF16)
    identf_d = din("identf", [128, 128])
    onesf_d = din("onesf", [128, 128])
    sinkrow_d = din("sinkrow", [1, 1536])
    wdw_d = din("wdw_t", [128, 8 * 31])
    selc_d = din("selc", [128, 8])
    vec_d = din("vec_t", [128, 24])
    ck_d = din("cache_k", [4, 128, 256])
    cv_d = din("cache_v", [4, 128, 256])
    sc_d = din("state_conv", [120, 1024])

    yp = dout("yp", [SEQ, D])
    ys = dout("ys", [128, D])
    nkp = dout("nkp", [128, 256])
    nvp = dout("nvp", [128, 256])
    ncp = dout("ncp", [30, 1024])
    nks = dout("nks", [4, 128, 256])
    nvs = dout("nvs", [4, 128, 256])
    ncs = dout("ncs", [4, 30, 1024])

    wbf_in = dscr("wbf_in", [NCH_IN, 128, 16 * 128], BF16)
    wbf_v = dscr("wbf_v", [128, 16 * 256], BF16)
    wbf_pa = dscr("wbf_pa", [16, 128, 8 * 128], BF16)
    wbf_pb = dscr("wbf_pb", [16, 128, 8 * 128], BF16)
    wbf_out = dscr("wbf_out", [8, 128, 16 * 256], BF16)
    wdiag = dscr("wdiag", [8, 128, 31 * 128], BF16)
    modscr = dscr("modscr", [2, 3, 128, D], F32)

    vnext = [PG * 32768]

    def sb(name, shape, dt):
        t = nc.alloc_sbuf_tensor(name, list(shape), dt)
        n = int(np.prod(shape[1:])) * _esz(dt)
        vb = vnext[0]
        vnext[0] += ((n + PG - 1) // PG + 1) * PG
        return Buf(t[:] if len(shape) == 2 else t[tuple(slice(None) for _ in shape)], vb, n)

    ARENA_A = 81920
    ARENA_B = 17408
    arenaA = nc.alloc_sbuf_tensor("arenaA", [128, ARENA_A // 4], F32)
    arenaB = nc.alloc_sbuf_tensor("arenaB", [128, ARENA_B // 4], F32)

    def carve(arena, abase, off, shape, dt):
        n = int(np.prod(shape[1:])) * _esz(dt)
        assert off % 4 == 0 and n % 4 == 0
        ap = arena[:, off // 4:(off + n) // 4]
        if dt == BF16:
            ap = ap.bitcast(BF16)
        if len(shape) == 3:
            ap = ap.rearrange("p (a b) -> p a b", a=shape[1])
        elif len(shape) == 4:
            ap = ap.rearrange("p (a b c) -> p a b c", a=shape[1], b=shape[2])
        return Buf(ap, abase + off, n)

    def cA(off, shape, dt):
        return carve(arenaA, 0, off, shape, dt)

    def cB(off, shape, dt):
        return carve(arenaB, PG * 8192, off, shape, dt)

    xbuf = sb("xbuf", [128, 4, D], F32)
    hT = sb("hT", [128, 16, NT], BF16)
    wring = sb("wring", [128, 4, 2048], BF16)
    kT2 = sb("kT2", [128, 4, 128 + NT], BF16)
    Vx = sb("Vx", [128, 8, 4, 128], BF16)
    arenaC = nc.alloc_sbuf_tensor("arenaC", [128, 4096], F32)
    ABASE_C = PG * 16384
    attn_n = carve(arenaC, ABASE_C, 0, [128, 8, NT], BF16)
    conv_g = carve(arenaC, ABASE_C, 8192, [128, 8, NT], BF16)
    xst = [carve(arenaC, ABASE_C, 0, [128, D], F32), carve(arenaC, ABASE_C, 8192, [128, D], F32)]
    Acol_p = sb("Acol_p", [128, 16], F32)
    Scol_p = sb("Scol_p", [128, 16], F32)
    Acol_s = sb("Acol_s", [128, 16, 4], F32)
    Scol_s = sb("Scol_s", [128, 16, 4], F32)
    selc = sb("selc_s", [128, 8], F32)
    statF = sb("statF", [128, 16], F32)
    kf32 = sb("kf32", [128, 4, 128], F32)
    rmat = sb("rmat_s", [128, 128], BF16)
    identb = sb("identb_s", [128, 128], BF16)
    onesb = sb("onesb_s", [128, 128], BF16)
    identf = sb("identf_s", [128, 128], F32)
    onesf = sb("onesf_s", [128, 128], F32)
    wdw = sb("wdw_s", [128, 8, 31], F32)
    vec = sb("vec_s", [128, 24], F32)
    sinkf = sb("sinkf", [1, 1536], F32)
    esink = sb("esink", [1, 1536], BF16)
    stat = sb("stat", [128, 32], F32)
    vf32 = sb("vf32", [128, 256], F32)
    kost = sb("kost", [128, 256], F32)

    wada = cA(0, [128, 2, 16 * 512], BF16)
    lhsP = cA(32768, [128, 16, 128], BF16)
    lhsS = cA(36864, [128, 16, 128], BF16)
    cstg = cA(40960, [128, 16, 128], F32)
    mstg = cA(49152, [128, 2, 512], F32)
    bstg = cA(53248, [128, 2, 512], F32)
    gstg = cA(57344, [128, 2, 512], F32)
    dstage = cA(61440, [128, 2, 31 * 128], BF16)
    modA = cA(0, [128, D], F32)
    modS = cA(8192, [128, D], F32)
    tmpx = cA(16384, [128, D], F32)
    h_tm = cA(24576, [128, 2, D], BF16)
    junk = cA(65536, [128, D], BF16)
    wv = cA(0, [128, 16, 256], BF16)
    cstage = cA(0, [128, 1024], F32)
    qT = cA(8192, [128, 8, NT], BF16)
    sga = cA(16384, [128, 8, NT], BF16)
    sgb = cA(24576, [128, 8, NT], BF16)
    acc = cA(32768, [128, 8, NT], F32)
    ysq = cA(49152, [128, NT], F32)
    msq = cA(51200, [128, NT], F32)
    rstdb = cA(53248, [128, NT], F32)
    ycen = cA(55296, [128, 2, NT], F32)
    PT = cA(59392, [128, 2, 512], BF16)
    rec = cA(61440, [128, 512], F32)
    PT2 = cA(73728, [128, 2, 512], BF16)
    rec2 = cA(75776, [128, 512], F32)
    meanb = cA(4096, [128, NT], F32)
    PTs = [PT, PT2]
    recs = [rec, rec2]
    cosb = cA(63488, [128, NT], F32)
    sinb = cA(65536, [128, NT], F32)
    qsb = cA(67584, [128, 2, NT], BF16)
    rt1 = cA(69632, [128, NT], F32)
    rt2 = cA(71680, [128, NT], F32)
    sig = cA(73728, [128, 2, NT], F32)
    u32 = cA(77824, [128, 8, 128], F32)
    modG = cA(0, [128, D], F32)
    modF = cA(8192, [128, D], F32)
    ybuf = cA(16384, [128, 2, D], F32)
    wout = cA(32768, [128, 2, 16 * 256], BF16)
    merged = cA(49152, [128, 16, NT], BF16)
    sAB = cA(65536, [128, 4, NT], F32)
    tAB = cA(73728, [128, 4, NT], F32)
    U = cB(0, [128, 2, 8 * 542], BF16)
    Us = cB(0, [128, 8, 4, 62], BF16)
    kTc = cB(4096, [128, 4, 4, 128], BF16)
    ckst = cB(8192, [128, 4, 128], F32)
    scst = cB(10240, [128, 1024], F32)

    PS = [nc.alloc_psum_tensor(f"ps{i}", [128, 512], F32) for i in range(8)]
    PSR = [[("psum", i)] for i in range(8)]

    def U_ap(b):
        return U.ap[:, b, :].rearrange("p (c t) -> p c t", c=8)

    def U_res(b, c=None):
        if c is None:
            return U.res(b * 8 * 542 * 2, (b + 1) * 8 * 542 * 2)
        return U.res((b * 8 + c) * 542 * 2, (b * 8 + c + 1) * 542 * 2)

    def dma(q, out, in_, reads, writes, is_out=False):
        return S.add(q, lambda e: e.dma_start(out=out, in_=in_), reads, writes, dma=True, is_out=is_out)

    def mm(out, lhsT, rhs, start, stop, reads, writes):
        return S.add("pe", lambda e: e.matmul(out, lhsT, rhs, start=start, stop=stop), reads, writes)

    def tr(out, in_, ident, reads, writes):
        return S.add("pe", lambda e: e.transpose(out, in_, ident), reads, writes)

    def act(out, in_, func, reads, writes, **kw):
        return S.add("act", lambda e: e.activation(out=out, in_=in_, func=func, **kw), reads, writes)

    def tt(eng, out, in0, in1, op, reads, writes):
        return S.add(eng, lambda e: e.tensor_tensor(out=out, in0=in0, in1=in1, op=op), reads, writes)

    def ts(eng, out, in0, s1, s2, op0, op1, reads, writes):
        return S.add(eng, lambda e: e.tensor_scalar(out=out, in0=in0, scalar1=s1, scalar2=s2, op0=op0, op1=op1),
                     reads, writes)

    def stt(eng, out, in0, scalar, in1, op0, op1, reads, writes):
        return S.add(eng, lambda e: e.scalar_tensor_tensor(out=out, in0=in0, scalar=scalar, in1=in1, op0=op0, op1=op1),
                     reads, writes)

    def cp(eng, out, in_, reads, writes):
        if eng == "act":
            return S.add("act", lambda e: e.copy(out=out, in_=in_), reads, writes)
        return S.add(eng, lambda e: e.tensor_copy(out=out, in_=in_), reads, writes)

    def recip(out, in_, reads, writes):
        return S.add("dve", lambda e: e.reciprocal(out=out, in_=in_), reads, writes)

    def memset(eng, ap, val, writes):
        return S.add(eng, lambda e: e.memset(ap, val), (), writes)

    def _construct():
        CG = 4
        for dst, src in ((rmat, rmat_d), (identb, identb_d), (onesb, onesb_d), (identf, identf_d),
                         (onesf, onesf_d), (vec, vec_d), (sinkf, sinkrow_d), (selc, selc_d)):
            dma("sp", dst.ap, src, (), dst.res())
        dma("sp", wdw.ap, wdw_d.rearrange("p (c w) -> p c w", c=8), (), wdw.res())
        act(esink.ap, sinkf.ap, AF.Exp, sinkf.res(), esink.res())

        for c in range(8):
            sl = c % 2
            st_ = dstage.ap[:, sl, :].rearrange("p (w n) -> p w n", w=31)
            tt("dve", st_, identb.ap.unsqueeze(1).to_broadcast([128, 31, 128]),
               wdw.ap[:, c, :].unsqueeze(2).to_broadcast([128, 31, 128]), ALU.mult,
               identb.res() + wdw.res(), dstage.cres(sl, 7936))
            dma("pool", wdiag[c], dstage.ap[:, sl, :], dstage.cres(sl, 7936), [("wdiag", c)])
        ckpt("setup")
        for grp, (csrc, lhs) in enumerate(((ctp, lhsP), (cts, lhsS))):
            dma("sp", cstg.ap, csrc, (), cstg.res())
            act(lhs.ap, cstg.ap, AF.Silu, cstg.res(), lhs.res())
        def modulation(n):
            slot = n % 2
            wslot = wada.ap[:, slot, :].rearrange("p (k n) -> p k n", k=16)
            dma("pool", wada.ap[:, slot, :], w_ada_t[n].rearrange("p k n -> p (k n)"), (), wada.cres(slot, 16384))
            dma("sp", bstg.ap[:, slot, :], b_ada_bc[:, n * 512:(n + 1) * 512], (), bstg.cres(slot, 2048))
            kind = n // 4
            if kind == 1:
                c0 = (n % 4) * 512
                dma("sp", gstg.ap[:, slot, :], g_bc[:, c0:c0 + 512], (), gstg.cres(slot, 2048))
            for grp, lhs in enumerate((lhsP, lhsS)):
                pb = n % 2 * 2 + grp
                for kc in range(16):
                    mm(PS[pb][:, :], lhs.ap[:, kc, :], wslot[:, kc, :], kc == 0, kc == 15,
                       lhs.res() + wada.cres(slot, 16384), PSR[pb])
                ms = mstg.ap[:, grp, :]
                tt("dve", ms, PS[pb][:, :], bstg.ap[:, slot, :], ALU.add,
                   PSR[pb] + bstg.cres(slot, 2048), mstg.cres(grp, 2048))
                if kind == 1:
                    stt("dve", ms, ms, 1.0, gstg.ap[:, slot, :], ALU.add, ALU.mult,
                        mstg.cres(grp, 2048) + gstg.cres(slot, 2048), mstg.cres(grp, 2048))
                if kind == 2:
                    c0 = (n % 4) * 512
                    dma("sp", modscr[grp, kind, :, c0:c0 + 512], ms, mstg.cres(grp, 2048), [("modscr", grp, kind, n % 4)])
                else:
                    pbc = 4 + grp
                    sel = selc.ap[:, 0:4] if grp == 0 else selc.ap[:, 4:8]
                    for j in range(4):
                        mm(PS[pbc][:, 4 * j:4 * j + 4], ms[:, j * 128:(j + 1) * 128], sel, True, True,
                           mstg.cres(grp, 2048) + selc.res(), PSR[pbc])
                    kc0 = (n % 4) * 4
                    pv4 = PS[pbc][:, 0:16].rearrange("p (j c) -> p j c", j=4)
                    if grp == 0:
                        dst = (Scol_p if kind == 0 else Acol_p)
                        cp("dve", dst.ap[:, kc0:kc0 + 4], pv4[:, :, 0], PSR[pbc], dst.res())
                    else:
                        dst = (Scol_s if kind == 0 else Acol_s)
                        cp("dve", dst.ap[:, kc0:kc0 + 4, :], pv4, PSR[pbc], dst.res())

        issued = set()

        def cast_item(item, gate):
            if item in issued:
                return
            issued.add(item)
            rd = [gate] if gate is not None else []
            kind, i = item
            if kind == "in":
                g0 = i * CG
                dma("pool", wbf_in[g0:g0 + CG], w_in_t[g0:g0 + CG], rd, [("wbfin", i)])
            elif kind == "v":
                dma("pool", wbf_v, w_v_t, rd, [("wbfv",)])
            elif kind == "pa":
                dma("pool", wbf_pa[4 * i:4 * i + 4], w_pa_t[4 * i:4 * i + 4], rd, [("wbfpa", i)])
            elif kind == "pb":
                dma("pool", wbf_pb[4 * i:4 * i + 4], w_pb_t[4 * i:4 * i + 4], rd, [("wbfpb", i)])
            elif kind == "out":
                dma("pool", wbf_out[2 * i:2 * i + 2], w_out_t[2 * i:2 * i + 2], rd, [("wbfout", i)])

        def cast_for_chunk(ci):
            gate = ("cg", ci - 1) if ci >= 1 else None
            g = ci // CG + 2
            if g <= 10:
                cast_item(("in", g), gate)
            if ci == 12:
                cast_item(("v", 0), gate)
            if ci == 36:
                for it in (("pa", 0), ("pb", 0), ("in", 11), ("in", 12)):
                    cast_item(it, gate)

        def cast_for_f(f):
            gate = ("cgf", f - 1) if f >= 1 else None
            plan = {1: [("pa", 1), ("pb", 1), ("in", 13), ("in", 14)],
                    5: [("pa", 2), ("pb", 2), ("in", 15), ("in", 16)],
                    9: [("pa", 3), ("pb", 3), ("in", 17), ("in", 18)],
                    12: [("out", 0), ("out", 1)],
                    14: [("out", 2), ("out", 3)]}
            for it in plan.get(f, ()):
                cast_item(it, gate)

        evcnt = [0]

        def tile_params(ti):
            sample = ti == NPT
            return sample, (xs if sample else xp[ti * NT:(ti + 1) * NT]), (1 if sample else 4)

        def in_a_front(ti, m):
            sample, xsrc, NS = tile_params(ti)
            xb = xst[m % 2]
            dma("sp", xb.ap, xsrc[m * 128:(m + 1) * 128, :], (), xb.res())
            act(junk.ap, xb.ap, AF.Square, xb.res(), junk.res() + stat.res(), accum_out=stat.ap[:, m:m + 1])
            ts("dve", stat.ap[:, 4 + m:5 + m], stat.ap[:, m:m + 1], 1.0 / D, RMS_EPS, ALU.mult, ALU.add,
               stat.res(), stat.res())
            act(stat.ap[:, 8 + m:9 + m], stat.ap[:, 4 + m:5 + m], AF.Sqrt, stat.res(), stat.res())
            recip(stat.ap[:, 12 + m:13 + m], stat.ap[:, 8 + m:9 + m], stat.res(), stat.res())
            ckpt("f_pre")
            ts("dve", xb.ap, xb.ap, stat.ap[:, 12 + m:13 + m], None, ALU.mult, ALU.bypass, xb.res() + stat.res(), xb.res())
            ckpt("f_post")

        def in_a_back(ti, m):
            sample, xsrc, NS = tile_params(ti)
            xb = xst[m % 2]
            for k0 in range(0, 16, 4):
                pb = 6 + (k0 // 4) % 2
                for kk in range(4):
                    kc = k0 + kk
                    tr(PS[pb][:, kk * 128:(kk + 1) * 128], xb.ap[:, kc * 128:(kc + 1) * 128], identf.ap,
                       xb.res() + identf.res(), PSR[pb])
                ckpt("b_tr")
                for kk in range(4):
                    kc = k0 + kk
                    hres = hT.res(kc * NT * 2, (kc + 1) * NT * 2)
                    if sample:
                        pieces = [(PS[pb][:, kk * 128 + 32 * s_:kk * 128 + 32 * s_ + 32], hT.ap[:, kc, 32 * s_:32 * s_ + 32],
                                   Acol_s.ap[:, kc, s_:s_ + 1], Scol_s.ap[:, kc, s_:s_ + 1]) for s_ in range(4)]
                        cres = Acol_s.res() + Scol_s.res()
                    else:
                        pieces = [(PS[pb][:, kk * 128:(kk + 1) * 128], hT.ap[:, kc, m * 128:(m + 1) * 128],
                                   Acol_p.ap[:, kc:kc + 1], Scol_p.ap[:, kc:kc + 1])]
                        cres = Acol_p.res() + Scol_p.res()
                    for (src, dst, a_, s_c) in pieces:
                        evcnt[0] += 1
                        ts("dve", dst, src, a_, s_c, ALU.mult, ALU.add, PSR[pb] + cres, hres)

        in_a_front(0, 0)
        in_a_front(0, 1)
        for n in range(12):
            modulation(n)
        cast_item(("in", 0), None)
        cast_item(("in", 1), None)
        ckpt("mod")
        wr_cnt = [0]

        def wslot_load(src_ap, nel, src_res):
            s = wr_cnt[0] % 4
            wr_cnt[0] += 1
            dma("sp", wring.ap[:, s, 0:nel], src_ap, src_res, wring.cres(s, 4096))
            return s

        psrot = [0]
        ropecnt = [0]
        rescnt = [0]

        def next_ps(n=4, base=0):
            i = base + psrot[0] % n
            psrot[0] += 1
            return i

        def conv_state_out(srcfn, dst, resname):
            for c in range(8):
                pbk = c // 4
                tr(PS[pbk][0:30, (c % 4) * 128:(c % 4 + 1) * 128], srcfn(c), identf.ap, u32.res() + identf.res(), PSR[pbk])
            for hb in range(2):
                cp("dve", cstage.ap[0:30, hb * 512:(hb + 1) * 512], PS[hb][0:30, :], PSR[hb], cstage.res())
            dma("sp", dst, cstage.ap[0:30, :], cstage.res(), [resname], is_out=True)

        def process_tile(ti):
            sample = ti == NPT
            N = 128 if sample else NT
            NS = N // 128
            grp = 1 if sample else 0
            xsrc = xs if sample else xp[ti * NT:(ti + 1) * NT]
            ydst = ys if sample else yp[ti * NT:(ti + 1) * NT]
            ub = ti % 2
            last_p = ti == NPT - 1

            if sample:
                dma("sp", cosb.ap[:, 0:N], cos_s, (), cosb.res())
                dma("sp", sinb.ap[:, 0:N], sin_s, (), sinb.res())
            else:
                dma("sp", cosb.ap, cos_p[:, ti * NT:(ti + 1) * NT], (), cosb.res())
                dma("sp", sinb.ap, sin_p[:, ti * NT:(ti + 1) * NT], (), sinb.res())
            ckpt(f"ina{ti}")
            def proj_chunk(ci, nb=4):
                if ti == 0:
                    cast_for_chunk(ci)
                s = wslot_load(wbf_in[ci], 2048, [("wbfin", ci // CG)])
                wsl = wring.ap[:, s, :].rearrange("p (k n) -> p k n", k=16)
                pb = next_ps(nb)
                for kc in range(16):
                    mm(PS[pb][:, 0:N], wsl[:, kc, :], hT.ap[:, kc, 0:N], kc == 0, kc == 15,
                       wring.cres(s, 4096) + hT.res(kc * NT * 2, (kc + 1) * NT * 2),
                       PSR[pb] + ([("cg", ci)] if (ti == 0 and kc == 15) else []))
                return pb

            if sample:
                u_cur = lambda c: Us.ap[:, c, :, 30:62]
                u_res = lambda c: Us.res()
            else:
                Uv = U_ap(ub)
                u_cur = lambda c: Uv[:, c, 30:30 + N]
                u_res = lambda c: U_res(ub, c)

            def as_tok(ap):
                return ap.rearrange("p (s t) -> p s t", s=4) if sample else ap

            def conv_chunk(c):
                s0 = wslot_load(wdiag[c, :, 0:2048], 2048, [("wdiag", c)])
                s1 = wslot_load(wdiag[c, :, 2048:3968], 1920, [("wdiag", c)])
                pb = 6 + c % 2
                groups = [(s_, Us.ap[:, c, s_, :], PS[pb][:, 32 * s_:32 * s_ + 32], 32, Us.res()) for s_ in range(4)] \
                    if sample else [(0, Uv[:, c, :], PS[pb][:, 0:N], N, U_res(ub, c))]
                for (_, src, dst, n_, rres) in groups:
                    for w in range(31):
                        sl_ = s0 if w < 16 else s1
                        wl = wring.ap[:, sl_, :].rearrange("p (k n) -> p k n", k=16)[:, w % 16, :]
                        mm(dst, wl, src[:, w:w + n_], w == 0, w == 30, wring.cres(sl_, 4096) + rres, PSR[pb])
                act(acc.ap[:, c, 0:N], PS[pb][:, 0:N], AF.Identity, PSR[pb] + vec.res(), acc.cres(c, NT * 4),
                    bias=vec.ap[:, c:c + 1])

            for c in range(8):
                pv_ = proj_chunk(2 * c, 8)
                pg_ = proj_chunk(2 * c + 1, 8)
                sl = c % 2
                act(sig.ap[:, sl, 0:N], PS[pg_][:, 0:N], AF.Sigmoid, PSR[pg_], sig.cres(sl, NT * 4))
                tt("dve", u_cur(c), as_tok(PS[pv_][:, 0:N]), as_tok(sig.ap[:, sl, 0:N]), ALU.mult,
                   PSR[pv_] + sig.cres(sl, NT * 4), u_res(c))
                if sample:
                    tt("dve", u32.ap[:, c, 0:128], PS[pv_][:, 0:128], sig.ap[:, sl, 0:128], ALU.mult,
                       PSR[pv_] + sig.cres(sl, NT * 4), u32.cres(c, 512))
                elif last_p:
                    tt("dve", u32.ap[:, c, 0:30], PS[pv_][:, N - 30:N], sig.ap[:, sl, N - 30:N], ALU.mult,
                       PSR[pv_] + sig.cres(sl, NT * 4), u32.cres(c, 512))
            ckpt(f"glu{ti}")
            conv_pending = list(range(8))

            def conv_some(n):
                if conv_pending:
                    conv_chunk(conv_pending.pop(0))

            def rope(pb, dst_ap, dst_res, f32_dst=None):
                sl = ropecnt[0] % 2
                ropecnt[0] += 1
                cp("act", qsb.ap[:, sl, 0:N], PS[pb][:, 0:N], PSR[pb], qsb.cres(sl, NT * 2))
                pr = next_ps(2, 4)
                mm(PS[pr][:, 0:N], rmat.ap, qsb.ap[:, sl, 0:N], True, True, rmat.res() + qsb.cres(sl, NT * 2), PSR[pr])
                tt("pool", rt1.ap[:, 0:N], qsb.ap[:, sl, 0:N], cosb.ap[:, 0:N], ALU.mult,
                   qsb.cres(sl, NT * 2) + cosb.res(), rt1.res())
                tt("dve", rt2.ap[:, 0:N], PS[pr][:, 0:N], sinb.ap[:, 0:N], ALU.mult, PSR[pr] + sinb.res(), rt2.res())
                tt("dve", dst_ap, rt1.ap[:, 0:N], rt2.ap[:, 0:N], ALU.add, rt1.res() + rt2.res(), dst_res)
                if f32_dst is not None:
                    tt("pool", f32_dst[0], rt1.ap[:, N - 128:N], rt2.ap[:, N - 128:N], ALU.add,
                       rt1.res() + rt2.res(), f32_dst[1])

            for g in range(4):
                pb = proj_chunk(16 + g)
                f32d = (kf32.ap[:, g, :], kf32.cres(g, 512)) if (sample or last_p) else None
                rope(pb, kT2.ap[:, g, 128:128 + N], kT2.cres(g, (128 + NT) * 2), f32d)
                conv_some(8)

            dma("sp", wv.ap, wbf_v.rearrange("p (k n) -> p k n", k=16), [("wbfv",)], wv.res())
            if sample:
                vblocks = [(s * 32, 32, 4 + s) for s in range(4)]
            else:
                vblocks = [(m * 128, 128, 1 + m) for m in range(NS)]
            for (t0, nt, blk) in vblocks:
                pb = next_ps()
                for kc in range(16):
                    mm(PS[pb][0:nt, 0:256], hT.ap[:, kc, t0:t0 + nt], wv.ap[:, kc, :], kc == 0, kc == 15,
                       hT.res(kc * NT * 2, (kc + 1) * NT * 2) + wv.res(), PSR[pb])
                pv3 = PS[pb][0:nt, 0:256].rearrange("p (g d) -> p g d", g=4)
                cp("act", Vx.ap[0:nt, blk, :, 0:64], pv3, PSR[pb], Vx.cres(blk, 1024))
                cp("dve", Vx.ap[0:nt, blk, :, 64:128], pv3, PSR[pb], Vx.cres(blk, 1024))
                if sample:
                    s = blk - 4
                    cp("dve", vf32.ap[0:32, :], PS[pb][0:32, 0:256], PSR[pb], vf32.res())
                    dma("sp", nvs[s, 96:128, :], vf32.ap[0:32, :], vf32.res(), [("nvs", s)], is_out=True)
                elif last_p and blk == NS:
                    cp("dve", vf32.ap, PS[pb][:, 0:256], PSR[pb], vf32.res())
                    dma("sp", nvp, vf32.ap, vf32.res(), [("nvp",)], is_out=True)
                conv_some(8)

            def layer_norm():
                sqb = [ysq, rstdb, Buf(ycen.ap[:, 0, :], ycen.vbase, NT * 4),
                       Buf(ycen.ap[:, 1, :], ycen.vbase + NT * 4, NT * 4)]

                def sq_op(c):
                    sq = sqb[c % 4]
                    act(sq.ap[:, 0:N], acc.ap[:, c, 0:N], AF.Square, acc.cres(c, NT * 4), sq.res())

                for c in range(4):
                    sq_op(c)
                for c in range(8):
                    mm(PS[0][:, 0:N], onesf.ap, acc.ap[:, c, 0:N], c == 0, c == 7,
                       onesf.res() + acc.cres(c, NT * 4), PSR[0])
                for c in range(8):
                    sq = sqb[c % 4]
                    mm(PS[1][:, 0:N], onesf.ap, sq.ap[:, 0:N], c == 0, c == 7, onesf.res() + sq.res(), PSR[1])
                    if c + 4 < 8:
                        sq_op(c + 4)
                act(msq.ap[:, 0:N], PS[0][:, 0:N], AF.Square, PSR[0], msq.res())
                cp("dve", meanb.ap[:, 0:N], PS[0][:, 0:N], PSR[0], meanb.res())
                stt("dve", rstdb.ap[:, 0:N], PS[1][:, 0:N], LN_EPS, msq.ap[:, 0:N], ALU.add, ALU.subtract,
                    PSR[1] + msq.res(), rstdb.res())
                act(rstdb.ap[:, 0:N], rstdb.ap[:, 0:N], AF.Sqrt, rstdb.res(), rstdb.res())
                recip(rstdb.ap[:, 0:N], rstdb.ap[:, 0:N], rstdb.res(), rstdb.res())
                for c in range(8):
                    sl = c % 2
                    yc = ycen.ap[:, sl, 0:N]
                    tt("dve", yc, acc.ap[:, c, 0:N], meanb.ap[:, 0:N], ALU.subtract, acc.cres(c, NT * 4) + meanb.res(),
                       ycen.cres(sl, NT * 4))
                    tt("pool", yc, yc, rstdb.ap[:, 0:N], ALU.mult, ycen.cres(sl, NT * 4) + rstdb.res(),
                       ycen.cres(sl, NT * 4))
                    act(conv_g.ap[:, c, 0:N], yc, AF.Silu, ycen.cres(sl, NT * 4) + vec.res(), conv_g.cres(c, NT * 2),
                        scale=vec.ap[:, 8 + c:9 + c], bias=vec.ap[:, 16 + c:17 + c])

            while conv_pending:
                conv_some(1)
            if not sample and not last_p:
                cp("pool", U_ap(1 - ub)[:, :, 0:30], U_ap(ub)[:, :, NT:NT + 30], U_res(ub), U_res(1 - ub))
            layer_norm()
            ckpt("lnB")

            for c in range(8):
                pb = proj_chunk(20 + c)
                rope(pb, qT.ap[:, c, 0:N], qT.cres(c, NT * 2))
                conv_some(8)
                ckpt(f"q{c}")
            ckpt(f"proj{ti}")
            def attn_A(ui, qcol0, nq, heads, segs, sink_off, out_tok0):
                nh = len(heads)
                ncol = nh * nq
                par = ui % 2
                stb = (4, 5) if par == 0 else (2, 3)
                PTb = PTs[par]
                hw = (nh // 2) * nq
                for si, (kfn, M, r0, r1, vblk, kres) in enumerate(segs):
                    for hi, h in enumerate(heads):
                        hp = h % 2
                        g = h // 4
                        pb = stb[hp]
                        c0_ = si * hw + (hi // 2) * nq
                        mm(PS[pb][0:M, c0_:c0_ + nq], kfn(g, hp),
                           qT.ap[hp * 64:(hp + 1) * 64, h // 2, qcol0:qcol0 + nq], True, True,
                           kres + qT.cres(h // 2, NT * 2), PSR[pb])
                for si, (kfn, M, r0, r1, vblk, kres) in enumerate(segs):
                    for hp in range(2):
                        pb = stb[hp]
                        act(PTb.ap[r0:r1, si, 0:ncol].rearrange("p (c two q) -> p c two q", two=2, q=nq)[:, :, hp, :],
                            PS[pb][r0:r1, si * hw:(si + 1) * hw].rearrange("p (c q) -> p c q", q=nq),
                            AF.Exp, PSR[pb], PTb.cres(si, 1024), scale=SCALE)

            def attn_B(ui, qcol0, nq, heads, segs, sink_off, out_tok0):
                nh = len(heads)
                ncol = nh * nq
                par = ui % 2
                PTb = PTs[par]
                recb = recs[par]
                ng = nh // 4
                for gi in range(ng):
                    g = heads[0] // 4 + gi
                    for si, (kfn, M, r0, r1, vblk, kres) in enumerate(segs):
                        mm(PS[6][:, gi * 4 * nq:(gi + 1) * 4 * nq], Vx.ap[r0:r1, vblk, g, :],
                           PTb.ap[r0:r1, si, gi * 4 * nq:(gi + 1) * 4 * nq], si == 0, si == len(segs) - 1,
                           Vx.cres(vblk, 1024) + PTb.cres(si, 1024), PSR[6])
                for si, (kfn, M, r0, r1, vblk, kres) in enumerate(segs):
                    mm(PS[7][:, 0:ncol], onesb.ap[r0:r1, :], PTb.ap[r0:r1, si, 0:ncol], si == 0, False,
                       onesb.res() + PTb.cres(si, 1024), PSR[7])
                mm(PS[7][:, 0:ncol], onesb.ap[0:1, :], esink.ap[0:1, sink_off:sink_off + ncol], False, True,
                   onesb.res() + esink.res(), PSR[7])
                act(recb.ap[:, 0:ncol], PS[7][:, 0:ncol], AF.Ln, PSR[7], recb.res())
                act(recb.ap[:, 0:ncol], recb.ap[:, 0:ncol], AF.Exp, recb.res(), recb.res(), scale=-1.0)
                c0 = heads[0] // 2
                for hp in range(2):
                    ov = PS[6][:, 0:ncol].rearrange("p (c two q) -> p c two q", two=2, q=nq)[hp * 64:(hp + 1) * 64, :, hp, :]
                    rv = recb.ap[:, 0:ncol].rearrange("p (c two q) -> p c two q", two=2, q=nq)[hp * 64:(hp + 1) * 64, :, hp, :]
                    tt("dve", attn_n.ap[hp * 64:(hp + 1) * 64, c0:c0 + nh // 2, out_tok0:out_tok0 + nq], ov, rv, ALU.mult,
                       PSR[6] + recb.res(), attn_n.res(c0 * NT * 2, (c0 + nh // 2) * NT * 2))

            units = []
            if sample:
                for s in range(4):
                    segs = [
                        (lambda g, hp, s=s: kTc.ap[hp * 64:(hp + 1) * 64, g, s, :], 128, 0, 128, s, kTc.res()),
                        (lambda g, hp, s=s: kT2.ap[hp * 64:(hp + 1) * 64, g, 128 + 32 * s:128 + 32 * s + 32], 32, 0, 32,
                         4 + s, kT2.res()),
                    ]
                    units.append((32 * s, 32, list(range(16)), segs, 1024, 32 * s))
            else:
                for jj in range(8):
                    J = ti * 8 + jj
                    b = jj // 2
                    segs = []
                    if jj % 2 == 0:
                        if J >= 2:
                            cA0 = 128 + 128 * (b - 1)
                            segs.append((lambda g, hp, c=cA0: kT2.ap[hp * 64:(hp + 1) * 64, g, c:c + 128], 128, 0, 128, b,
                                         kT2.res()))
                        cB0 = 128 + 128 * b
                        segs.append((lambda g, hp, c=cB0: kT2.ap[hp * 64:(hp + 1) * 64, g, c:c + 64], 64, 0, 64, b + 1,
                                     kT2.res()))
                    else:
                        if J >= 2:
                            cA0 = 128 + 128 * (b - 1)
                            segs.append((lambda g, hp, c=cA0: kT2.ap[hp * 64:(hp + 1) * 64, g, c:c + 128], 128, 64, 128, b,
                                         kT2.res()))
                        cB0 = 128 + 128 * b
                        segs.append((lambda g, hp, c=cB0: kT2.ap[hp * 64:(hp + 1) * 64, g, c:c + 128], 128, 0, 128, b + 1,
                                     kT2.res()))
                    for hf in range(2):
                        units.append((64 * jj, 64, list(range(8 * hf, 8 * hf + 8)), segs, 512 * hf, 64 * jj))

            gate_chunks = [(28 + c, sga, c) for c in range(8)] + [(36 + c, sgb, c) for c in range(8)]

            def gate_chunk():
                ci_, dstb, c = gate_chunks.pop(0)
                pb = next_ps(2, 0)
                if ti == 0:
                    cast_for_chunk(ci_)
                s_ = wslot_load(wbf_in[ci_], 2048, [("wbfin", ci_ // CG)])
                wsl = wring.ap[:, s_, :].rearrange("p (k n) -> p k n", k=16)
                for kc in range(16):
                    mm(PS[pb][:, 0:N], wsl[:, kc, :], hT.ap[:, kc, 0:N], kc == 0, kc == 15,
                       wring.cres(s_, 4096) + hT.res(kc * NT * 2, (kc + 1) * NT * 2),
                       PSR[pb] + ([("cg", ci_)] if (ti == 0 and kc == 15) else []))
                act(dstb.ap[:, c, 0:N], PS[pb][:, 0:N], AF.Silu, PSR[pb], dstb.cres(c, NT * 2))

            per_unit = (16 + len(units) - 1) // len(units)
            attn_A(0, *units[0])
            for ui, u in enumerate(units):
                if ui + 1 < len(units):
                    attn_A(ui + 1, *units[ui + 1])
                attn_B(ui, *u)
                for _ in range(per_unit):
                    if gate_chunks:
                        gate_chunk()
            while gate_chunks:
                gate_chunk()
            if not sample and not last_p:
                cp("pool", kT2.ap[:, :, 0:128], kT2.ap[:, :, NT:NT + 128], kT2.res(), kT2.res())
                cp("pool", Vx.ap[:, 0, :, :], Vx.ap[:, 4, :, :], Vx.cres(4, 1024), Vx.cres(0, 1024))
            for c in range(8):
                tt("pool", conv_g.ap[:, c, 0:N], conv_g.ap[:, c, 0:N], sgb.ap[:, c, 0:N], ALU.mult,
                   conv_g.cres(c, NT * 2) + sgb.cres(c, NT * 2), conv_g.cres(c, NT * 2))
            for c in range(8):
                tt("pool", attn_n.ap[:, c, 0:N], attn_n.ap[:, c, 0:N], sga.ap[:, c, 0:N], ALU.mult,
                   attn_n.cres(c, NT * 2) + sga.cres(c, NT * 2), attn_n.cres(c, NT * 2))

            ckpt(f"att{ti}")
            ckpt(f"ln{ti}")
            if sample or last_p:
                for g in range(4):
                    tr(PS[2][:, g * 128:(g + 1) * 128], kf32.ap[:, g, :], identf.ap, kf32.cres(g, 512) + identf.res(), PSR[2])
                cp("dve", kost.ap.rearrange("p (g d) -> p g d", g=4),
                   PS[2][:, :].rearrange("p (g d) -> p g d", g=4)[:, :, 0:64], PSR[2], kost.res())
                if sample:
                    for s in range(4):
                        dma("sp", nks[s, 96:128, :], kost.ap[32 * s:32 * s + 32, :], kost.res(), [("nks", s)], is_out=True)
                        dma("sp", nks[s, 0:96, :], ck_d[s, 32:128, :], (), [("nks0", s)], is_out=True)
                        dma("sp", nvs[s, 0:96, :], cv_d[s, 32:128, :], (), [("nvs0", s)], is_out=True)
                else:
                    dma("sp", nkp, kost.ap, kost.res(), [("nkp",)], is_out=True)
                if last_p:
                    conv_state_out(lambda c: u32.ap[:, c, 0:30], ncp, ("ncp",))
                else:
                    for s in range(4):
                        conv_state_out(lambda c, s=s: u32.ap[:, c, 32 * s + 2:32 * s + 32], ncs[s], ("ncs", s))

            ckpt(f"state{ti}")
            for m in range(NS):
                dma("sp", xbuf.ap[:, m, :], xsrc[m * 128:(m + 1) * 128, :], (), xbuf.cres(m, 8192))
            dma("sp", modG.ap, modscr[grp, 2], [("modscr", grp, 2, q) for q in range(4)], modG.res())
            dma("sp", modF.ap, fg_bc, (), modF.res())
            for f in range(16):
                S.tag = f"t{ti}-merge-f{f}"
                if ti == 0:
                    cast_for_f(f)
                base = 0 if f % 2 == 0 else 4
                sa = wslot_load(wbf_pa[f], 1024, [("wbfpa", f // 4)])
                sbb = wslot_load(wbf_pb[f], 1024, [("wbfpb", f // 4)])
                wa = wring.ap[:, sa, 0:1024].rearrange("p (k n) -> p k n", k=8)
                wb = wring.ap[:, sbb, 0:1024].rearrange("p (k n) -> p k n", k=8)
                for kc in range(8):
                    mm(PS[base][:, 0:N], wa[:, kc, :], attn_n.ap[:, kc, 0:N], kc == 0, kc == 7,
                       wring.cres(sa, 4096) + attn_n.cres(kc, NT * 2), PSR[base])
                for kc in range(8):
                    mm(PS[base + 1][:, 0:N], wb[:, kc, :], conv_g.ap[:, kc, 0:N], kc == 0, kc == 7,
                       wring.cres(sbb, 4096) + conv_g.cres(kc, NT * 2), PSR[base + 1])
                for ab in range(2):
                    s = wslot_load(wbf_in[44 + 2 * f + ab], 2048, [("wbfin", (44 + 2 * f + ab) // CG)])
                    wsl = wring.ap[:, s, :].rearrange("p (k n) -> p k n", k=16)
                    pbm = base + 2 + ab
                    for kc in range(16):
                        mm(PS[pbm][:, 0:N], wsl[:, kc, :], hT.ap[:, kc, 0:N], kc == 0, kc == 15,
                           wring.cres(s, 4096) + hT.res(kc * NT * 2, (kc + 1) * NT * 2),
                           PSR[pbm] + ([("cgf", f)] if (ti == 0 and ab == 1 and kc == 15) else []))
                    si = (f % 2) * 2 + ab
                    act(sAB.ap[:, si, 0:N], PS[pbm][:, 0:N], AF.Sigmoid, PSR[pbm], sAB.cres(si, NT * 4))
                sa_i = (f % 2) * 2
                tt("dve", tAB.ap[:, sa_i, 0:N], PS[base][:, 0:N], sAB.ap[:, sa_i, 0:N], ALU.mult,
                   PSR[base] + sAB.cres(sa_i, NT * 4), tAB.cres(sa_i, NT * 4))
                tt("dve", tAB.ap[:, sa_i + 1, 0:N], PS[base + 1][:, 0:N], sAB.ap[:, sa_i + 1, 0:N], ALU.mult,
                   PSR[base + 1] + sAB.cres(sa_i + 1, NT * 4), tAB.cres(sa_i + 1, NT * 4))
                tt("pool", merged.ap[:, f, 0:N], tAB.ap[:, sa_i, 0:N], tAB.ap[:, sa_i + 1, 0:N], ALU.add,
                   tAB.cres(sa_i, NT * 4) + tAB.cres(sa_i + 1, NT * 4), merged.cres(f, NT * 2))

            S.tag = f"t{ti}-wout"
            nxt = ti + 1 if ti + 1 <= NPT else None
            nNS = (1 if nxt == NPT else 4) if nxt is not None else 0
            if nxt is not None:
                in_a_front(nxt, 0)
                if nNS > 1:
                    in_a_front(nxt, 1)
            def wout_load(n_):
                dma("sp", wout.ap[:, n_ % 2, :], wbf_out[n_], [("wbfout", n_ // 2)], wout.cres(n_ % 2, 8192))

            wout_load(0)
            wout_load(1)
            for n in range(8):
                slot = n % 2
                wo = wout.ap[:, slot, :].rearrange("p (k n) -> p k n", k=16)
                for m in range(NS):
                    pb = next_ps(6)
                    for kc in range(16):
                        mm(PS[pb][:, 0:256], merged.ap[:, kc, m * 128:(m + 1) * 128], wo[:, kc, :], kc == 0, kc == 15,
                           merged.cres(kc, NT * 2) + wout.cres(slot, 8192), PSR[pb])
                    xs_ = xbuf.ap[:, m, n * 256:(n + 1) * 256]
                    rs = rescnt[0] % 4
                    rescnt[0] += 1
                    tt("dve", tAB.ap[:, rs, 0:256], PS[pb][:, 0:256], modG.ap[:, n * 256:(n + 1) * 256], ALU.mult,
                       PSR[pb] + modG.res(), tAB.cres(rs, NT * 4))
                    tt("pool", xs_, xs_, tAB.ap[:, rs, 0:256], ALU.add, xbuf.cres(m, 8192) + tAB.cres(rs, NT * 4),
                       xbuf.cres(m, 8192))
                if n + 2 < 8:
                    wout_load(n + 2)
                if nxt is not None and n % 2 == 1:
                    mb = n // 2
                    if mb < nNS:
                        in_a_back(nxt, mb)
                        if mb + 2 < nNS:
                            in_a_front(nxt, mb + 2)
            for m in range(NS):
                act(ybuf.ap[:, m % 2, :], xbuf.ap[:, m, :], AF.Square, xbuf.cres(m, 8192),
                    ybuf.cres(m % 2, 8192) + statF.res(), accum_out=statF.ap[:, m:m + 1])
                ts("dve", statF.ap[:, 4 + m:5 + m], statF.ap[:, m:m + 1], 1.0 / D, RMS_EPS, ALU.mult, ALU.add,
                   statF.res(), statF.res())
                act(statF.ap[:, 8 + m:9 + m], statF.ap[:, 4 + m:5 + m], AF.Sqrt, statF.res(), statF.res())
                recip(statF.ap[:, 12 + m:13 + m], statF.ap[:, 8 + m:9 + m], statF.res(), statF.res())
                stt("dve", ybuf.ap[:, m % 2, :], xbuf.ap[:, m, :], statF.ap[:, 12 + m:13 + m], modF.ap, ALU.mult, ALU.mult,
                    xbuf.cres(m, 8192) + statF.res() + modF.res(), ybuf.cres(m % 2, 8192))
                dma("pool", ydst[m * 128:(m + 1) * 128, :], ybuf.ap[:, m % 2, :], ybuf.cres(m % 2, 8192),
                    [("yout", ti, m)], is_out=True)

            ckpt(f"out{ti}")
        memset("pool", U_ap(0)[:, :, 0:30], 0.0, U_res(0))

        for m in range(4):
            in_a_back(0, m)
            if m + 2 < 4:
                in_a_front(0, m + 2)
        for ti in range(NPT):
            process_tile(ti)

        for s in range(4):
            dma("sp", ckst.ap[:, :, 0:64], ck_d[s].rearrange("k (g d) -> k g d", g=4), (), ckst.res())
            dma("sp", ckst.ap[:, :, 64:128], ck_d[s].rearrange("k (g d) -> k g d", g=4), (), ckst.res())
            for g in range(4):
                tr(PS[0][:, g * 128:(g + 1) * 128], ckst.ap[:, g, :], identf.ap, ckst.res() + identf.res(), PSR[0])
            cp("dve", kTc.ap[:, :, s, :], PS[0][:, :].rearrange("p (g k) -> p g k", g=4), PSR[0], kTc.res())
            dma("pool", Vx.ap[:, s, :, 0:64], cv_d[s].rearrange("k (g d) -> k g d", g=4), (), Vx.cres(s, 1024))
            dma("pool", Vx.ap[:, s, :, 64:128], cv_d[s].rearrange("k (g d) -> k g d", g=4), (), Vx.cres(s, 1024))
        dma("sp", scst.ap[0:120, :], sc_d, (), scst.res())
        for c in range(8):
            pb = c % 2
            tr(PS[pb][:, 0:120], scst.ap[0:120, c * 128:(c + 1) * 128], identf.ap[0:120, 0:120],
               scst.res() + identf.res(), PSR[pb])
            cp("dve" if c % 2 == 0 else "act", Us.ap[:, c, :, 0:30],
               PS[pb][:, 0:120].rearrange("p (s r) -> p s r", s=4), PSR[pb], Us.res())
        process_tile(NPT)

    try:
        _construct()
    except _StopBuild:
        pass

    fin = S.add("sp", lambda e: e.nop(), (), ())
    fin.deps = list(S.out_dmas)

    from contextlib import ExitStack
    with ExitStack() as st:
        sems = {k: st.enter_context(nc.semaphore("sem_" + k)) for k in ("pe", "act", "dve", "pool", "sp")}
        dma_sems = [st.enter_context(nc.semaphore(f"dsem{i}")) for i in range(N_DMA_SEMS)]
        S.finalize(nc, sems, dma_sems)
        block = st.enter_context(nc.Block())

        @block.sync
        def _(e):
            S.emit_engine("sp", e)

        @block.tensor
        def _(e):
            S.emit_engine("pe", e)

        @block.scalar
        def _(e):
            S.emit_engine("act", e)

        @block.vector
        def _(e):
            S.emit_engine("dve", e)

        @block.gpsimd
        def _(e):
            S.emit_engine("pool", e)
    return nc


_NC_CACHE = {}


def _rope_tables(pos):
    half = 32
    inv = (10000.0 ** (-2.0 * np.arange(half, dtype=np.float32) / 64.0)).astype(np.float32)
    ang = pos.astype(np.float32)[None, :] * inv[:, None]
    cos = np.cos(ang).astype(np.float32)
    sin = np.sin(ang).astype(np.float32)
    return np.ascontiguousarray(np.tile(cos, (4, 1))), np.ascontiguousarray(np.tile(sin, (4, 1)))


def prep_inputs(x_prompt, x_sample, c_prompt, c_sample, cache_k, cache_v, state_conv,
                norm_g, w_ada, b_ada, w_in, sinks, w_dw, b_dw, ln_g, ln_b,
                w_proj_a, w_proj_b, w_out, final_g):
    f32 = np.float32
    A = lambda a: np.ascontiguousarray(np.asarray(a, dtype=f32))
    x_prompt, x_sample = A(x_prompt), A(x_sample)
    w_in0 = A(w_in)[0]
    cols = []
    for c in range(8):
        cols.append(np.arange(2560 + 128 * c, 2560 + 128 * c + 128))
        cols.append(np.arange(3584 + 128 * c, 3584 + 128 * c + 128))
    for g in range(4):
        k = np.arange(1024 + 64 * g, 1024 + 64 * g + 64)
        cols.append(np.concatenate([k, k]))
    for c in range(8):
        cols.append(np.arange(128 * c, 128 * c + 128))
    for c in range(8):
        cols.append(np.arange(1536 + 128 * c, 1536 + 128 * c + 128))
    for c in range(8):
        cols.append(np.arange(4608 + 128 * c, 4608 + 128 * c + 128))
    for f in range(16):
        cols.append(np.arange(5632 + 128 * f, 5632 + 128 * f + 128))
        cols.append(np.arange(7680 + 128 * f, 7680 + 128 * f + 128))
    colidx = np.concatenate(cols)
    assert colidx.shape[0] == NCH_IN * 128
    W = w_in0[:, colidx].reshape(16, 128, NCH_IN, 128)
    w_in_t = np.ascontiguousarray(W.transpose(2, 1, 0, 3)).reshape(NCH_IN, 128, 2048)
    w_v_t = np.ascontiguousarray(w_in0[:, 1280:1536].reshape(16, 128, 256).transpose(1, 0, 2)).reshape(128, 4096)

    def proj_t(w):
        w = A(w)[0].reshape(8, 128, 16, 128)
        return np.ascontiguousarray(w.transpose(2, 1, 0, 3)).reshape(16, 128, 1024)

    w_pa_t, w_pb_t = proj_t(w_proj_a), proj_t(w_proj_b)
    w_out_t = np.ascontiguousarray(A(w_out)[0].reshape(16, 128, 8, 256).transpose(2, 1, 0, 3)).reshape(8, 128, 4096)
    w_ada_t = np.ascontiguousarray(A(w_ada)[0].reshape(16, 128, 12, 512).transpose(2, 1, 0, 3))
    b_ada_bc = np.ascontiguousarray(np.broadcast_to(A(b_ada)[0][None, :], (128, 6144)))
    g_bc = np.ascontiguousarray(np.broadcast_to(A(norm_g)[0][None, :], (128, D)))
    fg_bc = np.ascontiguousarray(np.broadcast_to(A(final_g)[None, :], (128, D)))
    cos_p, sin_p = _rope_tables(np.arange(SEQ))
    cos_s1, sin_s1 = _rope_tables(1024 + np.arange(32))
    cos_s = np.ascontiguousarray(np.tile(cos_s1, (1, 4)))
    sin_s = np.ascontiguousarray(np.tile(sin_s1, (1, 4)))
    rmat = np.zeros((128, 128), f32)
    for m in range(128):
        if (m % 64) < 32:
            rmat[m + 32, m] = -1.0
        else:
            rmat[m - 32, m] = 1.0
    bf = ml_dtypes.bfloat16
    sk = A(sinks)[0]
    sinkrow = np.concatenate([np.repeat(sk, 64), np.repeat(sk, 32)])[None, :].astype(f32)
    wdw_t = np.ascontiguousarray(A(w_dw)[0].reshape(31, 8, 128).transpose(2, 1, 0)).reshape(128, 248)
    vt = lambda v: A(v)[0].reshape(8, 128).T
    vec_t = np.ascontiguousarray(np.concatenate([vt(b_dw), vt(ln_g), vt(ln_b)], axis=1))
    selc = np.zeros((128, 8), f32)
    selc[:, 0:4] = 1.0 / 128.0
    for s_ in range(4):
        selc[32 * s_:32 * s_ + 32, 4 + s_] = 1.0 / 32.0
    common = dict(
        w_ada_t=w_ada_t, b_ada_bc=b_ada_bc, g_bc=g_bc, fg_bc=fg_bc, w_in_t=w_in_t, w_v_t=w_v_t,
        w_pa_t=w_pa_t, w_pb_t=w_pb_t, w_out_t=w_out_t, cos_p=cos_p, sin_p=sin_p, cos_s=cos_s, sin_s=sin_s,
        rmat=rmat.astype(bf), identb=np.eye(128, dtype=f32).astype(bf), onesb=np.ones((128, 128), f32).astype(bf),
        identf=np.eye(128, dtype=f32), onesf=np.full((128, 128), 1.0 / 1024.0, f32), sinkrow=sinkrow,
        wdw_t=wdw_t, vec_t=vec_t, selc=selc,
    )
    c_prompt, c_sample = A(c_prompt), A(c_sample)
    ck, cv, scv = A(cache_k)[0], A(cache_v)[0], A(state_conv)[0]
    in_maps = []
    for c in range(8):
        cp_ = c_prompt[c].reshape(16, 128).T
        ctp = np.ascontiguousarray(np.broadcast_to(cp_[:, :, None], (128, 16, 128)))
        cs_ = c_sample[4 * c:4 * c + 4].reshape(4, 16, 128)
        cts = np.ascontiguousarray(np.repeat(cs_.transpose(2, 1, 0), 32, axis=2))
        m = dict(common)
        m.update(
            xp=x_prompt[c], xs=np.ascontiguousarray(x_sample[4 * c:4 * c + 4].reshape(128, D)),
            ctp=ctp, cts=cts,
            cache_k=np.ascontiguousarray(ck[4 * c:4 * c + 4].reshape(4, 128, 256)),
            cache_v=np.ascontiguousarray(cv[4 * c:4 * c + 4].reshape(4, 128, 256)),
            state_conv=np.ascontiguousarray(scv[4 * c:4 * c + 4].reshape(120, 1024)),
        )
        in_maps.append(m)
    return in_maps


def kernel(**inputs):
    f32 = np.float32
    in_maps = prep_inputs(**inputs)
    if "nc" not in _NC_CACHE:
        _NC_CACHE["nc"] = build_nc()
    nc = _NC_CACHE["nc"]
    res = run_bass_kernel_spmd(nc, in_maps, core_ids=list(range(8)))
    R = res.results
    y_prompt = np.stack([R[c]["yp"] for c in range(8)]).astype(f32)
    y_sample = np.concatenate([R[c]["ys"].reshape(4, 32, D) for c in range(8)]).astype(f32)
    nkp = np.stack([R[c]["nkp"].reshape(128, 4, 64) for c in range(8)])[None].astype(f32)
    nvp = np.stack([R[c]["nvp"].reshape(128, 4, 64) for c in range(8)])[None].astype(f32)
    ncp = np.stack([R[c]["ncp"] for c in range(8)])[None].astype(f32)
    nks = np.concatenate([R[c]["nks"].reshape(4, 128, 4, 64) for c in range(8)])[None].astype(f32)
    nvs = np.concatenate([R[c]["nvs"].reshape(4, 128, 4, 64) for c in range(8)])[None].astype(f32)
    ncs = np.concatenate([R[c]["ncs"] for c in range(8)])[None].astype(f32)
    return (y_prompt, y_sample, nkp, nvp, ncp, nks, nvs, ncs)
```

```python
import numpy as np
import ml_dtypes
import concourse.bass as bass
import concourse.mybir as mybir
from concourse.bass_utils import run_bass_kernel_spmd

F32 = mybir.dt.float32
BF16 = mybir.dt.bfloat16
AF = mybir.ActivationFunctionType
ALU = mybir.AluOpType

D = 2048
SEQ = 4096
NT = 512
NPT = SEQ // NT
NCH_IN = 76
PG = 512
N_DMA_SEMS = 44
RMS_EPS = 1e-6
LN_EPS = 1e-5
SCALE = 0.125
LNB0, LNB1 = 6, 7


class Op:
    __slots__ = ("eng", "fn", "deps", "is_dma", "sem", "val", "signals", "idx", "tag")


class Sched:
    def __init__(self):
        self.ops = []
        self.lastw = {}
        self.readers = {}
        self.out_dmas = []
        self.tag = None

    def add(self, eng, fn, reads=(), writes=(), dma=False, is_out=False):
        op = Op()
        op.eng = eng
        op.fn = fn
        op.is_dma = dma
        op.signals = dma
        op.sem = None
        op.val = 0
        op.idx = len(self.ops)
        op.tag = self.tag
        deps = {}
        for r in reads:
            w = self.lastw.get(r)
            if w is not None:
                deps[w.idx] = w
        for r in writes:
            w = self.lastw.get(r)
            if w is not None:
                deps[w.idx] = w
            for rd in self.readers.get(r, ()):
                deps[rd.idx] = rd
        for r in reads:
            self.readers.setdefault(r, []).append(op)
        for r in writes:
            self.lastw[r] = op
            self.readers[r] = []
        red = {}
        for d in deps.values():
            if d.is_dma:
                red[("dma", d.idx)] = d
            else:
                k = d.eng
                if k not in red or red[k].idx < d.idx:
                    red[k] = d
        op.deps = list(red.values())
        self.ops.append(op)
        if is_out:
            self.out_dmas.append(op)
        return op

    def finalize(self, nc, sems, dma_sems):
        pools = {"pool": dma_sems[:12], "sp": dma_sems[12:]}
        state = {q: {"cnt": [0] * len(p), "last": [None] * len(p), "k": 0} for q, p in pools.items()}
        for op in self.ops:
            if op.is_dma:
                p = pools[op.eng]
                stq = state[op.eng]
                i = stq["k"] % len(p)
                stq["k"] += 1
                if stq["last"][i] is not None:
                    op.deps.append(stq["last"][i])
                stq["last"][i] = op
                stq["cnt"][i] += 1
                op.sem = p[i]
                op.val = 16 * stq["cnt"][i]
        for op in self.ops:
            for d in op.deps:
                if (not d.is_dma) and d.eng == "pe" and op.eng == "pe" and not op.is_dma:
                    continue
                d.signals = True
        ecnt = {}
        for op in self.ops:
            if not op.is_dma:
                op.sem = sems[op.eng]
                if op.signals:
                    ecnt[op.eng] = ecnt.get(op.eng, 0) + 1
                    op.val = ecnt[op.eng]

    def emit_engine(self, eng, e):
        waited = {}
        for op in self.ops:
            if op.eng != eng:
                continue
            needs = {}
            for d in op.deps:
                if (not d.is_dma) and d.eng == "pe" and eng == "pe" and not op.is_dma:
                    continue
                key = id(d.sem)
                if key not in needs or needs[key][1] < d.val:
                    needs[key] = (d.sem, d.val)
            for key, (s, v) in needs.items():
                if waited.get(key, 0) < v:
                    e.wait_ge(s, v)
                    waited[key] = v
            ins = op.fn(e)
            if op.signals:
                ins.then_inc(op.sem, 16 if op.is_dma else 1)


class Buf:
    def __init__(self, ap, vbase, nbytes):
        self.ap = ap
        self.vbase = vbase
        self.nbytes = nbytes

    def res(self, lo=0, hi=None):
        if hi is None:
            hi = self.nbytes
        return [("pg", p) for p in range((self.vbase + lo) // PG, (self.vbase + hi - 1) // PG + 1)]

    def cres(self, i, cbytes, n=1):
        return self.res(i * cbytes, (i + n) * cbytes)


def _esz(dt):
    return 2 if dt == BF16 else 4


class _StopBuild(Exception):
    pass


def build_nc(stop=None):
    nc = bass.Bass("TRN2", target_bir_lowering=False)
    S = Sched()

    def ckpt(name):
        if stop is not None and name == stop:
            raise _StopBuild()

    def din(name, shape, dt=F32):
        return nc.dram_tensor(name, list(shape), dt, kind="ExternalInput").ap()

    def dout(name, shape):
        return nc.dram_tensor(name, list(shape), F32, kind="ExternalOutput").ap()

    def dscr(name, shape, dt):
        return nc.dram_tensor(name, list(shape), dt, kind="Internal").ap()

    xp = din("xp", [SEQ, D])
    xs = din("xs", [128, D])
    ctp = din("ctp", [128, 16, 128])
    cts = din("cts", [128, 16, 128])
    w_ada_t = din("w_ada_t", [12, 128, 16, 512])
    b_ada_bc = din("b_ada_bc", [128, 6144])
    g_bc = din("g_bc", [128, D])
    fg_bc = din("fg_bc", [128, D])
    w_in_t = din("w_in_t", [NCH_IN, 128, 16 * 128])
    w_v_t = din("w_v_t", [128, 16 * 256])
    w_pa_t = din("w_pa_t", [16, 128, 8 * 128])
    w_pb_t = din("w_pb_t", [16, 128, 8 * 128])
    w_out_t = din("w_out_t", [8, 128, 16 * 256])
    cos_p = din("cos_p", [128, SEQ])
    sin_p = din("sin_p", [128, SEQ])
    cos_s = din("cos_s", [128, 128])
    sin_s = din("sin_s", [128, 128])
    rmat_d = din("rmat", [128, 128], BF16)
    identb_d = din("identb", [128, 128], BF16)
    onesb_d = din("onesb", [128, 128], BF16)
    identf_d = din("identf", [128, 128])
    onesf_d = din("onesf", [128, 128])
    sinkrow_d = din("sinkrow", [1, 1536])
    wdw_d = din("wdw_t", [128, 8 * 31])
    selc_d = din("selc", [128, 8])
    vec_d = din("vec_t", [128, 24])
    ck_d = din("cache_k", [4, 128, 256])
    cv_d = din("cache_v", [4, 128, 256])
    sc_d = din("state_conv", [120, 1024])

    yp = dout("yp", [SEQ, D])
    ys = dout("ys", [128, D])
    nkp = dout("nkp", [128, 256])
    nvp = dout("nvp", [128, 256])
    ncp = dout("ncp", [30, 1024])
    nks = dout("nks", [4, 128, 256])
    nvs = dout("nvs", [4, 128, 256])
    ncs = dout("ncs", [4, 30, 1024])

    wbf_in = dscr("wbf_in", [NCH_IN, 128, 16 * 128], BF16)
    wbf_v = dscr("wbf_v", [128, 16 * 256], BF16)
    wbf_pa = dscr("wbf_pa", [16, 128, 8 * 128], BF16)
    wbf_pb = dscr("wbf_pb", [16, 128, 8 * 128], BF16)
    wbf_out = dscr("wbf_out", [8, 128, 16 * 256], BF16)
    wdiag = dscr("wdiag", [8, 128, 31 * 128], BF16)
    modscr = dscr("modscr", [2, 3, 128, D], F32)

    vnext = [PG * 32768]

    def sb(name, shape, dt):
        t = nc.alloc_sbuf_tensor(name, list(shape), dt)
        n = int(np.prod(shape[1:])) * _esz(dt)
        vb = vnext[0]
        vnext[0] += ((n + PG - 1) // PG + 1) * PG
        return Buf(t[:] if len(shape) == 2 else t[tuple(slice(None) for _ in shape)], vb, n)

    ARENA_A = 81920
    ARENA_B = 17408
    arenaA = nc.alloc_sbuf_tensor("arenaA", [128, ARENA_A // 4], F32)
    arenaB = nc.alloc_sbuf_tensor("arenaB", [128, ARENA_B // 4], F32)

    def carve(arena, abase, off, shape, dt):
        n = int(np.prod(shape[1:])) * _esz(dt)
        assert off % 4 == 0 and n % 4 == 0
        ap = arena[:, off // 4:(off + n) // 4]
        if dt == BF16:
            ap = ap.bitcast(BF16)
        if len(shape) == 3:
            ap = ap.rearrange("p (a b) -> p a b", a=shape[1])
        elif len(shape) == 4:
            ap = ap.rearrange("p (a b c) -> p a b c", a=shape[1], b=shape[2])
        return Buf(ap, abase + off, n)

    def cA(off, shape, dt):
        return carve(arenaA, 0, off, shape, dt)

    def cB(off, shape, dt):
        return carve(arenaB, PG * 8192, off, shape, dt)

    xbuf = sb("xbuf", [128, 4, D], F32)
    hT = sb("hT", [128, 16, NT], BF16)
    wring = sb("wring", [128, 4, 2048], BF16)
    kT2 = sb("kT2", [128, 4, 128 + NT], BF16)
    Vx = sb("Vx", [128, 8, 4, 128], BF16)
    arenaC = nc.alloc_sbuf_tensor("arenaC", [128, 4096], F32)
    ABASE_C = PG * 16384
    attn_n = carve(arenaC, ABASE_C, 0, [128, 8, NT], BF16)
    conv_g = carve(arenaC, ABASE_C, 8192, [128, 8, NT], BF16)
    xst = [carve(arenaC, ABASE_C, 0, [128, D], F32), carve(arenaC, ABASE_C, 8192, [128, D], F32)]
    Acol_p = sb("Acol_p", [128, 16], F32)
    Scol_p = sb("Scol_p", [128, 16], F32)
    Acol_s = sb("Acol_s", [128, 16, 4], F32)
    Scol_s = sb("Scol_s", [128, 16, 4], F32)
    selc = sb("selc_s", [128, 8], F32)
    statF = sb("statF", [128, 16], F32)
    kf32 = sb("kf32", [128, 4, 128], F32)
    rmat = sb("rmat_s", [128, 128], BF16)
    identb = sb("identb_s", [128, 128], BF16)
    onesb = sb("onesb_s", [128, 128], BF16)
    identf = sb("identf_s", [128, 128], F32)
    onesf = sb("onesf_s", [128, 128], F32)
    wdw = sb("wdw_s", [128, 8, 31], F32)
    vec = sb("vec_s", [128, 24], F32)
    sinkf = sb("sinkf", [1, 1536], F32)
    esink = sb("esink", [1, 1536], BF16)
    stat = sb("stat", [128, 32], F32)
    vf32 = sb("vf32", [128, 256], F32)
    kost = sb("kost", [128, 256], F32)

    wada = cA(0, [128, 2, 16 * 512], BF16)
    lhsP = cA(32768, [128, 16, 128], BF16)
    lhsS = cA(36864, [128, 16, 128], BF16)
    cstg = cA(40960, [128, 16, 128], F32)
    mstg = cA(49152, [128, 2, 512], F32)
    bstg = cA(53248, [128, 2, 512], F32)
    gstg = cA(57344, [128, 2, 512], F32)
    dstage = cA(61440, [128, 2, 31 * 128], BF16)
    modA = cA(0, [128, D], F32)
    modS = cA(8192, [128, D], F32)
    tmpx = cA(16384, [128, D], F32)
    h_tm = cA(24576, [128, 2, D], BF16)
    junk = cA(65536, [128, D], BF16)
    wv = cA(0, [128, 16, 256], BF16)
    cstage = cA(0, [128, 1024], F32)
    qT = cA(8192, [128, 8, NT], BF16)
    sga = cA(16384, [128, 8, NT], BF16)
    sgb = cA(24576, [128, 8, NT], BF16)
    acc = cA(32768, [128, 8, NT], F32)
    ysq = cA(49152, [128, NT], F32)
    msq = cA(51200, [128, NT], F32)
    rstdb = cA(53248, [128, NT], F32)
    ycen = cA(55296, [128, 2, NT], F32)
    PT = cA(59392, [128, 2, 512], BF16)
    rec = cA(61440, [128, 512], F32)
    PT2 = cA(73728, [128, 2, 512], BF16)
    rec2 = cA(75776, [128, 512], F32)
    meanb = cA(4096, [128, NT], F32)
    PTs = [PT, PT2]
    recs = [rec, rec2]
    cosb = cA(63488, [128, NT], F32)
    sinb = cA(65536, [128, NT], F32)
    qsb = cA(67584, [128, 2, NT], BF16)
    rt1 = cA(69632, [128, NT], F32)
    rt2 = cA(71680, [128, NT], F32)
    sig = cA(73728, [128, 2, NT], F32)
    u32 = cA(77824, [128, 8, 128], F32)
    modG = cA(0, [128, D], F32)
    modF = cA(8192, [128, D], F32)
    ybuf = cA(16384, [128, 2, D], F32)
    wout = cA(32768, [128, 2, 16 * 256], BF16)
    merged = cA(49152, [128, 16, NT], BF16)
    sAB = cA(65536, [128, 4, NT], F32)
    tAB = cA(73728, [128, 4, NT], F32)
    U = cB(0, [128, 2, 8 * 542], BF16)
    Us = cB(0, [128, 8, 4, 62], BF16)
    kTc = cB(4096, [128, 4, 4, 128], BF16)
    ckst = cB(8192, [128, 4, 128], F32)
    scst = cB(10240, [128, 1024], F32)

    PS = [nc.alloc_psum_tensor(f"ps{i}", [128, 512], F32) for i in range(8)]
    PSR = [[("psum", i)] for i in range(8)]

    def U_ap(b):
        return U.ap[:, b, :].rearrange("p (c t) -> p c t", c=8)

    def U_res(b, c=None):
        if c is None:
            return U.res(b * 8 * 542 * 2, (b + 1) * 8 * 542 * 2)
        return U.res((b * 8 + c) * 542 * 2, (b * 8 + c + 1) * 542 * 2)

    def dma(q, out, in_, reads, writes, is_out=False):
        return S.add(q, lambda e: e.dma_start(out=out, in_=in_), reads, writes, dma=True, is_out=is_out)

    def mm(out, lhsT, rhs, start, stop, reads, writes):
        return S.add("pe", lambda e: e.matmul(out, lhsT, rhs, start=start, stop=stop), reads, writes)

    def tr(out, in_, ident, reads, writes):
        return S.add("pe", lambda e: e.transpose(out, in_, ident), reads, writes)

    def act(out, in_, func, reads, writes, **kw):
        return S.add("act", lambda e: e.activation(out=out, in_=in_, func=func, **kw), reads, writes)

    def tt(eng, out, in0, in1, op, reads, writes):
        return S.add(eng, lambda e: e.tensor_tensor(out=out, in0=in0, in1=in1, op=op), reads, writes)

    def ts(eng, out, in0, s1, s2, op0, op1, reads, writes):
        return S.add(eng, lambda e: e.tensor_scalar(out=out, in0=in0, scalar1=s1, scalar2=s2, op0=op0, op1=op1),
                     reads, writes)

    def stt(eng, out, in0, scalar, in1, op0, op1, reads, writes):
        return S.add(eng, lambda e: e.scalar_tensor_tensor(out=out, in0=in0, scalar=scalar, in1=in1, op0=op0, op1=op1),
                     reads, writes)

    def cp(eng, out, in_, reads, writes):
        if eng == "act":
            return S.add("act", lambda e: e.copy(out=out, in_=in_), reads, writes)
        return S.add(eng, lambda e: e.tensor_copy(out=out, in_=in_), reads, writes)

    def recip(out, in_, reads, writes):
        return S.add("dve", lambda e: e.reciprocal(out=out, in_=in_), reads, writes)

    def memset(eng, ap, val, writes):
        return S.add(eng, lambda e: e.memset(ap, val), (), writes)

    def _construct():
        CG = 4
        for dst, src in ((rmat, rmat_d), (identb, identb_d), (onesb, onesb_d), (identf, identf_d),
                         (onesf, onesf_d), (vec, vec_d), (sinkf, sinkrow_d), (selc, selc_d)):
            dma("sp", dst.ap, src, (), dst.res())
        dma("sp", wdw.ap, wdw_d.rearrange("p (c w) -> p c w", c=8), (), wdw.res())
        act(esink.ap, sinkf.ap, AF.Exp, sinkf.res(), esink.res())

        for c in range(8):
            sl = c % 2
            st_ = dstage.ap[:, sl, :].rearrange("p (w n) -> p w n", w=31)
            tt("dve", st_, identb.ap.unsqueeze(1).to_broadcast([128, 31, 128]),
               wdw.ap[:, c, :].unsqueeze(2).to_broadcast([128, 31, 128]), ALU.mult,
               identb.res() + wdw.res(), dstage.cres(sl, 7936))
            dma("pool", wdiag[c], dstage.ap[:, sl, :], dstage.cres(sl, 7936), [("wdiag", c)])
        ckpt("setup")
        for grp, (csrc, lhs) in enumerate(((ctp, lhsP), (cts, lhsS))):
            dma("sp", cstg.ap, csrc, (), cstg.res())
            act(lhs.ap, cstg.ap, AF.Silu, cstg.res(), lhs.res())
        def modulation(n):
            slot = n % 2
            wslot = wada.ap[:, slot, :].rearrange("p (k n) -> p k n", k=16)
            dma("pool", wada.ap[:, slot, :], w_ada_t[n].rearrange("p k n -> p (k n)"), (), wada.cres(slot, 16384))
            dma("sp", bstg.ap[:, slot, :], b_ada_bc[:, n * 512:(n + 1) * 512], (), bstg.cres(slot, 2048))
            kind = n // 4
            if kind == 1:
                c0 = (n % 4) * 512
                dma("sp", gstg.ap[:, slot, :], g_bc[:, c0:c0 + 512], (), gstg.cres(slot, 2048))
            for grp, lhs in enumerate((lhsP, lhsS)):
                pb = n % 2 * 2 + grp
                for kc in range(16):
                    mm(PS[pb][:, :], lhs.ap[:, kc, :], wslot[:, kc, :], kc == 0, kc == 15,
                       lhs.res() + wada.cres(slot, 16384), PSR[pb])
                ms = mstg.ap[:, grp, :]
                tt("dve", ms, PS[pb][:, :], bstg.ap[:, slot, :], ALU.add,
                   PSR[pb] + bstg.cres(slot, 2048), mstg.cres(grp, 2048))
                if kind == 1:
                    stt("dve", ms, ms, 1.0, gstg.ap[:, slot, :], ALU.add, ALU.mult,
                        mstg.cres(grp, 2048) + gstg.cres(slot, 2048), mstg.cres(grp, 2048))
                if kind == 2:
                    c0 = (n % 4) * 512
                    dma("sp", modscr[grp, kind, :, c0:c0 + 512], ms, mstg.cres(grp, 2048), [("modscr", grp, kind, n % 4)])
                else:
                    pbc = 4 + grp
                    sel = selc.ap[:, 0:4] if grp == 0 else selc.ap[:, 4:8]
                    for j in range(4):
                        mm(PS[pbc][:, 4 * j:4 * j + 4], ms[:, j * 128:(j + 1) * 128], sel, True, True,
                           mstg.cres(grp, 2048) + selc.res(), PSR[pbc])
                    kc0 = (n % 4) * 4
                    pv4 = PS[pbc][:, 0:16].rearrange("p (j c) -> p j c", j=4)
                    if grp == 0:
                        dst = (Scol_p if kind == 0 else Acol_p)
                        cp("dve", dst.ap[:, kc0:kc0 + 4], pv4[:, :, 0], PSR[pbc], dst.res())
                    else:
                        dst = (Scol_s if kind == 0 else Acol_s)
                        cp("dve", dst.ap[:, kc0:kc0 + 4, :], pv4, PSR[pbc], dst.res())

        issued = set()

        def cast_item(item, gate):
            if item in issued:
                return
            issued.add(item)
            rd = [gate] if gate is not None else []
            kind, i = item
            if kind == "in":
                g0 = i * CG
                dma("pool", wbf_in[g0:g0 + CG], w_in_t[g0:g0 + CG], rd, [("wbfin", i)])
            elif kind == "v":
                dma("pool", wbf_v, w_v_t, rd, [("wbfv",)])
            elif kind == "pa":
                dma("pool", wbf_pa[4 * i:4 * i + 4], w_pa_t[4 * i:4 * i + 4], rd, [("wbfpa", i)])
            elif kind == "pb":
                dma("pool", wbf_pb[4 * i:4 * i + 4], w_pb_t[4 * i:4 * i + 4], rd, [("wbfpb", i)])
            elif kind == "out":
                dma("pool", wbf_out[2 * i:2 * i + 2], w_out_t[2 * i:2 * i + 2], rd, [("wbfout", i)])

        def cast_for_chunk(ci):
            gate = ("cg", ci - 1) if ci >= 1 else None
            g = ci // CG + 2
            if g <= 10:
                cast_item(("in", g), gate)
            if ci == 12:
                cast_item(("v", 0), gate)
            if ci == 36:
                for it in (("pa", 0), ("pb", 0), ("in", 11), ("in", 12)):
                    cast_item(it, gate)

        def cast_for_f(f):
            gate = ("cgf", f - 1) if f >= 1 else None
            plan = {1: [("pa", 1), ("pb", 1), ("in", 13), ("in", 14)],
                    5: [("pa", 2), ("pb", 2), ("in", 15), ("in", 16)],
                    9: [("pa", 3), ("pb", 3), ("in", 17), ("in", 18)],
                    12: [("out", 0), ("out", 1)],
                    14: [("out", 2), ("out", 3)]}
            for it in plan.get(f, ()):
                cast_item(it, gate)

        evcnt = [0]

        def tile_params(ti):
            sample = ti == NPT
            return sample, (xs if sample else xp[ti * NT:(ti + 1) * NT]), (1 if sample else 4)

        def in_a_front(ti, m):
            sample, xsrc, NS = tile_params(ti)
            xb = xst[m % 2]
            dma("sp", xb.ap, xsrc[m * 128:(m + 1) * 128, :], (), xb.res())
            act(junk.ap, xb.ap, AF.Square, xb.res(), junk.res() + stat.res(), accum_out=stat.ap[:, m:m + 1])
            ts("dve", stat.ap[:, 4 + m:5 + m], stat.ap[:, m:m + 1], 1.0 / D, RMS_EPS, ALU.mult, ALU.add,
               stat.res(), stat.res())
            act(stat.ap[:, 8 + m:9 + m], stat.ap[:, 4 + m:5 + m], AF.Sqrt, stat.res(), stat.res())
            recip(stat.ap[:, 12 + m:13 + m], stat.ap[:, 8 + m:9 + m], stat.res(), stat.res())
            ckpt("f_pre")
            ts("dve", xb.ap, xb.ap, stat.ap[:, 12 + m:13 + m], None, ALU.mult, ALU.bypass, xb.res() + stat.res(), xb.res())
            ckpt("f_post")

        def in_a_back(ti, m):
            sample, xsrc, NS = tile_params(ti)
            xb = xst[m % 2]
            for k0 in range(0, 16, 4):
                pb = 6 + (k0 // 4) % 2
                for kk in range(4):
                    kc = k0 + kk
                    tr(PS[pb][:, kk * 128:(kk + 1) * 128], xb.ap[:, kc * 128:(kc + 1) * 128], identf.ap,
                       xb.res() + identf.res(), PSR[pb])
                ckpt("b_tr")
                for kk in range(4):
                    kc = k0 + kk
                    hres = hT.res(kc * NT * 2, (kc + 1) * NT * 2)
                    if sample:
                        pieces = [(PS[pb][:, kk * 128 + 32 * s_:kk * 128 + 32 * s_ + 32], hT.ap[:, kc, 32 * s_:32 * s_ + 32],
                                   Acol_s.ap[:, kc, s_:s_ + 1], Scol_s.ap[:, kc, s_:s_ + 1]) for s_ in range(4)]
                        cres = Acol_s.res() + Scol_s.res()
                    else:
                        pieces = [(PS[pb][:, kk * 128:(kk + 1) * 128], hT.ap[:, kc, m * 128:(m + 1) * 128],
                                   Acol_p.ap[:, kc:kc + 1], Scol_p.ap[:, kc:kc + 1])]
                        cres = Acol_p.res() + Scol_p.res()
                    for (src, dst, a_, s_c) in pieces:
                        evcnt[0] += 1
                        ts("dve", dst, src, a_, s_c, ALU.mult, ALU.add, PSR[pb] + cres, hres)

        in_a_front(0, 0)
        in_a_front(0, 1)
        for n in range(12):
            modulation(n)
        cast_item(("in", 0), None)
        cast_item(("in", 1), None)
        ckpt("mod")
        wr_cnt = [0]

        def wslot_load(src_ap, nel, src_res):
            s = wr_cnt[0] % 4
            wr_cnt[0] += 1
            dma("sp", wring.ap[:, s, 0:nel], src_ap, src_res, wring.cres(s, 4096))
            return s

        psrot = [0]
        ropecnt = [0]
        rescnt = [0]

        def next_ps(n=4, base=0):
            i = base + psrot[0] % n
            psrot[0] += 1
            return i

        def conv_state_out(srcfn, dst, resname):
            for c in range(8):
                pbk = c // 4
                tr(PS[pbk][0:30, (c % 4) * 128:(c % 4 + 1) * 128], srcfn(c), identf.ap, u32.res() + identf.res(), PSR[pbk])
            for hb in range(2):
                cp("dve", cstage.ap[0:30, hb * 512:(hb + 1) * 512], PS[hb][0:30, :], PSR[hb], cstage.res())
            dma("sp", dst, cstage.ap[0:30, :], cstage.res(), [resname], is_out=True)

        def process_tile(ti):
            sample = ti == NPT
            N = 128 if sample else NT
            NS = N // 128
            grp = 1 if sample else 0
            xsrc = xs if sample else xp[ti * NT:(ti + 1) * NT]
            ydst = ys if sample else yp[ti * NT:(ti + 1) * NT]
            ub = ti % 2
            last_p = ti == NPT - 1

            if sample:
                dma("sp", cosb.ap[:, 0:N], cos_s, (), cosb.res())
                dma("sp", sinb.ap[:, 0:N], sin_s, (), sinb.res())
            else:
                dma("sp", cosb.ap, cos_p[:, ti * NT:(ti + 1) * NT], (), cosb.res())
                dma("sp", sinb.ap, sin_p[:, ti * NT:(ti + 1) * NT], (), sinb.res())
            ckpt(f"ina{ti}")
            def proj_chunk(ci, nb=4):
                if ti == 0:
                    cast_for_chunk(ci)
                s = wslot_load(wbf_in[ci], 2048, [("wbfin", ci // CG)])
                wsl = wring.ap[:, s, :].rearrange("p (k n) -> p k n", k=16)
                pb = next_ps(nb)
                for kc in range(16):
                    mm(PS[pb][:, 0:N], wsl[:, kc, :], hT.ap[:, kc, 0:N], kc == 0, kc == 15,
                       wring.cres(s, 4096) + hT.res(kc * NT * 2, (kc + 1) * NT * 2),
                       PSR[pb] + ([("cg", ci)] if (ti == 0 and kc == 15) else []))
                return pb

            if sample:
                u_cur = lambda c: Us.ap[:, c, :, 30:62]
                u_res = lambda c: Us.res()
            else:
                Uv = U_ap(ub)
                u_cur = lambda c: Uv[:, c, 30:30 + N]
                u_res = lambda c: U_res(ub, c)

            def as_tok(ap):
                return ap.rearrange("p (s t) -> p s t", s=4) if sample else ap

            def conv_chunk(c):
                s0 = wslot_load(wdiag[c, :, 0:2048], 2048, [("wdiag", c)])
                s1 = wslot_load(wdiag[c, :, 2048:3968], 1920, [("wdiag", c)])
                pb = next_ps()
                groups = [(s_, Us.ap[:, c, s_, :], PS[pb][:, 32 * s_:32 * s_ + 32], 32, Us.res()) for s_ in range(4)] \
                    if sample else [(0, Uv[:, c, :], PS[pb][:, 0:N], N, U_res(ub, c))]
                for (_, src, dst, n_, rres) in groups:
                    for w in range(31):
                        sl_ = s0 if w < 16 else s1
                        wl = wring.ap[:, sl_, :].rearrange("p (k n) -> p k n", k=16)[:, w % 16, :]
                        mm(dst, wl, src[:, w:w + n_], w == 0, w == 30, wring.cres(sl_, 4096) + rres, PSR[pb])
                act(acc.ap[:, c, 0:N], PS[pb][:, 0:N], AF.Identity, PSR[pb] + vec.res(), acc.cres(c, NT * 4),
                    bias=vec.ap[:, c:c + 1])

            for c in range(8):
                pv_ = proj_chunk(2 * c, 8)
                pg_ = proj_chunk(2 * c + 1, 8)
                sl = c % 2
                act(sig.ap[:, sl, 0:N], PS[pg_][:, 0:N], AF.Sigmoid, PSR[pg_], sig.cres(sl, NT * 4))
                tt("dve", u_cur(c), as_tok(PS[pv_][:, 0:N]), as_tok(sig.ap[:, sl, 0:N]), ALU.mult,
                   PSR[pv_] + sig.cres(sl, NT * 4), u_res(c))
                if sample:
                    tt("dve", u32.ap[:, c, 0:128], PS[pv_][:, 0:128], sig.ap[:, sl, 0:128], ALU.mult,
                       PSR[pv_] + sig.cres(sl, NT * 4), u32.cres(c, 512))
                elif last_p:
                    tt("dve", u32.ap[:, c, 0:30], PS[pv_][:, N - 30:N], sig.ap[:, sl, N - 30:N], ALU.mult,
                       PSR[pv_] + sig.cres(sl, NT * 4), u32.cres(c, 512))
            ckpt(f"glu{ti}")
            conv_pending = list(range(8))

            def conv_some(n):
                if conv_pending:
                    conv_chunk(conv_pending.pop(0))

            def rope(pb, dst_ap, dst_res, f32_dst=None):
                sl = ropecnt[0] % 2
                ropecnt[0] += 1
                cp("act", qsb.ap[:, sl, 0:N], PS[pb][:, 0:N], PSR[pb], qsb.cres(sl, NT * 2))
                pr = next_ps(2, 4)
                mm(PS[pr][:, 0:N], rmat.ap, qsb.ap[:, sl, 0:N], True, True, rmat.res() + qsb.cres(sl, NT * 2), PSR[pr])
                tt("pool", rt1.ap[:, 0:N], qsb.ap[:, sl, 0:N], cosb.ap[:, 0:N], ALU.mult,
                   qsb.cres(sl, NT * 2) + cosb.res(), rt1.res())
                tt("dve", rt2.ap[:, 0:N], PS[pr][:, 0:N], sinb.ap[:, 0:N], ALU.mult, PSR[pr] + sinb.res(), rt2.res())
                tt("dve", dst_ap, rt1.ap[:, 0:N], rt2.ap[:, 0:N], ALU.add, rt1.res() + rt2.res(), dst_res)
                if f32_dst is not None:
                    tt("pool", f32_dst[0], rt1.ap[:, N - 128:N], rt2.ap[:, N - 128:N], ALU.add,
                       rt1.res() + rt2.res(), f32_dst[1])

            for g in range(4):
                pb = proj_chunk(16 + g)
                f32d = (kf32.ap[:, g, :], kf32.cres(g, 512)) if (sample or last_p) else None
                rope(pb, kT2.ap[:, g, 128:128 + N], kT2.cres(g, (128 + NT) * 2), f32d)
                conv_some(8)

            dma("sp", wv.ap, wbf_v.rearrange("p (k n) -> p k n", k=16), [("wbfv",)], wv.res())
            if sample:
                vblocks = [(s * 32, 32, 4 + s) for s in range(4)]
            else:
                vblocks = [(m * 128, 128, 1 + m) for m in range(NS)]
            for (t0, nt, blk) in vblocks:
                pb = next_ps()
                for kc in range(16):
                    mm(PS[pb][0:nt, 0:256], hT.ap[:, kc, t0:t0 + nt], wv.ap[:, kc, :], kc == 0, kc == 15,
                       hT.res(kc * NT * 2, (kc + 1) * NT * 2) + wv.res(), PSR[pb])
                pv3 = PS[pb][0:nt, 0:256].rearrange("p (g d) -> p g d", g=4)
                cp("act", Vx.ap[0:nt, blk, :, 0:64], pv3, PSR[pb], Vx.cres(blk, 1024))
                cp("dve", Vx.ap[0:nt, blk, :, 64:128], pv3, PSR[pb], Vx.cres(blk, 1024))
                if sample:
                    s = blk - 4
                    cp("dve", vf32.ap[0:32, :], PS[pb][0:32, 0:256], PSR[pb], vf32.res())
                    dma("sp", nvs[s, 96:128, :], vf32.ap[0:32, :], vf32.res(), [("nvs", s)], is_out=True)
                elif last_p and blk == NS:
                    cp("dve", vf32.ap, PS[pb][:, 0:256], PSR[pb], vf32.res())
                    dma("sp", nvp, vf32.ap, vf32.res(), [("nvp",)], is_out=True)
                conv_some(8)

            def layer_norm():
                sqb = [ysq, rstdb, Buf(ycen.ap[:, 0, :], ycen.vbase, NT * 4),
                       Buf(ycen.ap[:, 1, :], ycen.vbase + NT * 4, NT * 4)]

                def sq_op(c):
                    sq = sqb[c % 4]
                    act(sq.ap[:, 0:N], acc.ap[:, c, 0:N], AF.Square, acc.cres(c, NT * 4), sq.res())

                for c in range(4):
                    sq_op(c)
                for c in range(8):
                    mm(PS[0][:, 0:N], onesf.ap, acc.ap[:, c, 0:N], c == 0, c == 7,
                       onesf.res() + acc.cres(c, NT * 4), PSR[0])
                for c in range(8):
                    sq = sqb[c % 4]
                    mm(PS[1][:, 0:N], onesf.ap, sq.ap[:, 0:N], c == 0, c == 7, onesf.res() + sq.res(), PSR[1])
                    if c + 4 < 8:
                        sq_op(c + 4)
                act(msq.ap[:, 0:N], PS[0][:, 0:N], AF.Square, PSR[0], msq.res())
                cp("dve", meanb.ap[:, 0:N], PS[0][:, 0:N], PSR[0], meanb.res())
                stt("dve", rstdb.ap[:, 0:N], PS[1][:, 0:N], LN_EPS, msq.ap[:, 0:N], ALU.add, ALU.subtract,
                    PSR[1] + msq.res(), rstdb.res())
                act(rstdb.ap[:, 0:N], rstdb.ap[:, 0:N], AF.Sqrt, rstdb.res(), rstdb.res())
                recip(rstdb.ap[:, 0:N], rstdb.ap[:, 0:N], rstdb.res(), rstdb.res())
                for c in range(8):
                    sl = c % 2
                    yc = ycen.ap[:, sl, 0:N]
                    tt("dve", yc, acc.ap[:, c, 0:N], meanb.ap[:, 0:N], ALU.subtract, acc.cres(c, NT * 4) + meanb.res(),
                       ycen.cres(sl, NT * 4))
                    tt("pool", yc, yc, rstdb.ap[:, 0:N], ALU.mult, ycen.cres(sl, NT * 4) + rstdb.res(),
                       ycen.cres(sl, NT * 4))
                    act(conv_g.ap[:, c, 0:N], yc, AF.Silu, ycen.cres(sl, NT * 4) + vec.res(), conv_g.cres(c, NT * 2),
                        scale=vec.ap[:, 8 + c:9 + c], bias=vec.ap[:, 16 + c:17 + c])

            while conv_pending:
                conv_some(1)
            if not sample and not last_p:
                cp("pool", U_ap(1 - ub)[:, :, 0:30], U_ap(ub)[:, :, NT:NT + 30], U_res(ub), U_res(1 - ub))
            layer_norm()
            ckpt("lnB")

            for c in range(8):
                pb = proj_chunk(20 + c, 8)
                rope(pb, qT.ap[:, c, 0:N], qT.cres(c, NT * 2))
                conv_some(8)
                ckpt(f"q{c}")
            ckpt(f"proj{ti}")
            def attn_A(ui, qcol0, nq, heads, segs, sink_off, out_tok0):
                nh = len(heads)
                ncol = nh * nq
                par = ui % 2
                stb = (4, 5) if par == 0 else (2, 3)
                PTb = PTs[par]
                hw = (nh // 2) * nq
                for si, (kfn, M, r0, r1, vblk, kres) in enumerate(segs):
                    for hi, h in enumerate(heads):
                        hp = h % 2
                        g = h // 4
                        pb = stb[hp]
                        c0_ = si * hw + (hi // 2) * nq
                        mm(PS[pb][0:M, c0_:c0_ + nq], kfn(g, hp),
                           qT.ap[hp * 64:(hp + 1) * 64, h // 2, qcol0:qcol0 + nq], True, True,
                           kres + qT.cres(h // 2, NT * 2), PSR[pb])
                for si, (kfn, M, r0, r1, vblk, kres) in enumerate(segs):
                    for hp in range(2):
                        pb = stb[hp]
                        act(PTb.ap[r0:r1, si, 0:ncol].rearrange("p (c two q) -> p c two q", two=2, q=nq)[:, :, hp, :],
                            PS[pb][r0:r1, si * hw:(si + 1) * hw].rearrange("p (c q) -> p c q", q=nq),
                            AF.Exp, PSR[pb], PTb.cres(si, 1024), scale=SCALE)

            def attn_B(ui, qcol0, nq, heads, segs, sink_off, out_tok0):
                nh = len(heads)
                ncol = nh * nq
                par = ui % 2
                PTb = PTs[par]
                recb = recs[par]
                ng = nh // 4
                for gi in range(ng):
                    g = heads[0] // 4 + gi
                    for si, (kfn, M, r0, r1, vblk, kres) in enumerate(segs):
                        mm(PS[6][:, gi * 4 * nq:(gi + 1) * 4 * nq], Vx.ap[r0:r1, vblk, g, :],
                           PTb.ap[r0:r1, si, gi * 4 * nq:(gi + 1) * 4 * nq], si == 0, si == len(segs) - 1,
                           Vx.cres(vblk, 1024) + PTb.cres(si, 1024), PSR[6])
                for si, (kfn, M, r0, r1, vblk, kres) in enumerate(segs):
                    mm(PS[7][:, 0:ncol], onesb.ap[r0:r1, :], PTb.ap[r0:r1, si, 0:ncol], si == 0, False,
                       onesb.res() + PTb.cres(si, 1024), PSR[7])
                mm(PS[7][:, 0:ncol], onesb.ap[0:1, :], esink.ap[0:1, sink_off:sink_off + ncol], False, True,
                   onesb.res() + esink.res(), PSR[7])
                act(recb.ap[:, 0:ncol], PS[7][:, 0:ncol], AF.Ln, PSR[7], recb.res())
                act(recb.ap[:, 0:ncol], recb.ap[:, 0:ncol], AF.Exp, recb.res(), recb.res(), scale=-1.0)
                c0 = heads[0] // 2
                for hp in range(2):
                    ov = PS[6][:, 0:ncol].rearrange("p (c two q) -> p c two q", two=2, q=nq)[hp * 64:(hp + 1) * 64, :, hp, :]
                    rv = recb.ap[:, 0:ncol].rearrange("p (c two q) -> p c two q", two=2, q=nq)[hp * 64:(hp + 1) * 64, :, hp, :]
                    tt("dve", attn_n.ap[hp * 64:(hp + 1) * 64, c0:c0 + nh // 2, out_tok0:out_tok0 + nq], ov, rv, ALU.mult,
                       PSR[6] + recb.res(), attn_n.res(c0 * NT * 2, (c0 + nh // 2) * NT * 2))

            units = []
            if sample:
                for s in range(4):
                    segs = [
                        (lambda g, hp, s=s: kTc.ap[hp * 64:(hp + 1) * 64, g, s, :], 128, 0, 128, s, kTc.res()),
                        (lambda g, hp, s=s: kT2.ap[hp * 64:(hp + 1) * 64, g, 128 + 32 * s:128 + 32 * s + 32], 32, 0, 32,
                         4 + s, kT2.res()),
                    ]
                    units.append((32 * s, 32, list(range(16)), segs, 1024, 32 * s))
            else:
                for jj in range(8):
                    J = ti * 8 + jj
                    b = jj // 2
                    segs = []
                    if jj % 2 == 0:
                        if J >= 2:
                            cA0 = 128 + 128 * (b - 1)
                            segs.append((lambda g, hp, c=cA0: kT2.ap[hp * 64:(hp + 1) * 64, g, c:c + 128], 128, 0, 128, b,
                                         kT2.res()))
                        cB0 = 128 + 128 * b
                        segs.append((lambda g, hp, c=cB0: kT2.ap[hp * 64:(hp + 1) * 64, g, c:c + 64], 64, 0, 64, b + 1,
                                     kT2.res()))
                    else:
                        if J >= 2:
                            cA0 = 128 + 128 * (b - 1)
                            segs.append((lambda g, hp, c=cA0: kT2.ap[hp * 64:(hp + 1) * 64, g, c:c + 128], 128, 64, 128, b,
                                         kT2.res()))
                        cB0 = 128 + 128 * b
                        segs.append((lambda g, hp, c=cB0: kT2.ap[hp * 64:(hp + 1) * 64, g, c:c + 128], 128, 0, 128, b + 1,
                                     kT2.res()))
                    for hf in range(2):
                        units.append((64 * jj, 64, list(range(8 * hf, 8 * hf + 8)), segs, 512 * hf, 64 * jj))

            gate_chunks = [(28 + c, sga, c) for c in range(8)] + [(36 + c, sgb, c) for c in range(8)]

            def gate_chunk():
                ci_, dstb, c = gate_chunks.pop(0)
                pb = next_ps(2, 0)
                if ti == 0:
                    cast_for_chunk(ci_)
                s_ = wslot_load(wbf_in[ci_], 2048, [("wbfin", ci_ // CG)])
                wsl = wring.ap[:, s_, :].rearrange("p (k n) -> p k n", k=16)
                for kc in range(16):
                    mm(PS[pb][:, 0:N], wsl[:, kc, :], hT.ap[:, kc, 0:N], kc == 0, kc == 15,
                       wring.cres(s_, 4096) + hT.res(kc * NT * 2, (kc + 1) * NT * 2),
                       PSR[pb] + ([("cg", ci_)] if (ti == 0 and kc == 15) else []))
                act(dstb.ap[:, c, 0:N], PS[pb][:, 0:N], AF.Silu, PSR[pb], dstb.cres(c, NT * 2))

            per_unit = (16 + len(units) - 1) // len(units)
            attn_A(0, *units[0])
            for ui, u in enumerate(units):
                if ui + 1 < len(units):
                    attn_A(ui + 1, *units[ui + 1])
                attn_B(ui, *u)
                for _ in range(per_unit):
                    if gate_chunks:
                        gate_chunk()
            while gate_chunks:
                gate_chunk()
            if not sample and not last_p:
                cp("pool", kT2.ap[:, :, 0:128], kT2.ap[:, :, NT:NT + 128], kT2.res(), kT2.res())
                cp("pool", Vx.ap[:, 0, :, :], Vx.ap[:, 4, :, :], Vx.cres(4, 1024), Vx.cres(0, 1024))
            for c in range(8):
                tt("pool", conv_g.ap[:, c, 0:N], conv_g.ap[:, c, 0:N], sgb.ap[:, c, 0:N], ALU.mult,
                   conv_g.cres(c, NT * 2) + sgb.cres(c, NT * 2), conv_g.cres(c, NT * 2))
            for c in range(8):
                tt("pool", attn_n.ap[:, c, 0:N], attn_n.ap[:, c, 0:N], sga.ap[:, c, 0:N], ALU.mult,
                   attn_n.cres(c, NT * 2) + sga.cres(c, NT * 2), attn_n.cres(c, NT * 2))

            ckpt(f"att{ti}")
            ckpt(f"ln{ti}")
            if sample or last_p:
                for g in range(4):
                    tr(PS[2][:, g * 128:(g + 1) * 128], kf32.ap[:, g, :], identf.ap, kf32.cres(g, 512) + identf.res(), PSR[2])
                cp("dve", kost.ap.rearrange("p (g d) -> p g d", g=4),
                   PS[2][:, :].rearrange("p (g d) -> p g d", g=4)[:, :, 0:64], PSR[2], kost.res())
                if sample:
                    for s in range(4):
                        dma("sp", nks[s, 96:128, :], kost.ap[32 * s:32 * s + 32, :], kost.res(), [("nks", s)], is_out=True)
                        dma("sp", nks[s, 0:96, :], ck_d[s, 32:128, :], (), [("nks0", s)], is_out=True)
                        dma("sp", nvs[s, 0:96, :], cv_d[s, 32:128, :], (), [("nvs0", s)], is_out=True)
                else:
                    dma("sp", nkp, kost.ap, kost.res(), [("nkp",)], is_out=True)
                if last_p:
                    conv_state_out(lambda c: u32.ap[:, c, 0:30], ncp, ("ncp",))
                else:
                    for s in range(4):
                        conv_state_out(lambda c, s=s: u32.ap[:, c, 32 * s + 2:32 * s + 32], ncs[s], ("ncs", s))

            ckpt(f"state{ti}")
            for m in range(NS):
                dma("sp", xbuf.ap[:, m, :], xsrc[m * 128:(m + 1) * 128, :], (), xbuf.cres(m, 8192))
            dma("sp", modG.ap, modscr[grp, 2], [("modscr", grp, 2, q) for q in range(4)], modG.res())
            dma("sp", modF.ap, fg_bc, (), modF.res())
            for f in range(16):
                S.tag = f"t{ti}-merge-f{f}"
                if ti == 0:
                    cast_for_f(f)
                base = 0 if f % 2 == 0 else 4
                sa = wslot_load(wbf_pa[f], 1024, [("wbfpa", f // 4)])
                sbb = wslot_load(wbf_pb[f], 1024, [("wbfpb", f // 4)])
                wa = wring.ap[:, sa, 0:1024].rearrange("p (k n) -> p k n", k=8)
                wb = wring.ap[:, sbb, 0:1024].rearrange("p (k n) -> p k n", k=8)
                for kc in range(8):
                    mm(PS[base][:, 0:N], wa[:, kc, :], attn_n.ap[:, kc, 0:N], kc == 0, kc == 7,
                       wring.cres(sa, 4096) + attn_n.cres(kc, NT * 2), PSR[base])
                for kc in range(8):
                    mm(PS[base + 1][:, 0:N], wb[:, kc, :], conv_g.ap[:, kc, 0:N], kc == 0, kc == 7,
                       wring.cres(sbb, 4096) + conv_g.cres(kc, NT * 2), PSR[base + 1])
                for ab in range(2):
                    s = wslot_load(wbf_in[44 + 2 * f + ab], 2048, [("wbfin", (44 + 2 * f + ab) // CG)])
                    wsl = wring.ap[:, s, :].rearrange("p (k n) -> p k n", k=16)
                    pbm = base + 2 + ab
                    for kc in range(16):
                        mm(PS[pbm][:, 0:N], wsl[:, kc, :], hT.ap[:, kc, 0:N], kc == 0, kc == 15,
                           wring.cres(s, 4096) + hT.res(kc * NT * 2, (kc + 1) * NT * 2),
                           PSR[pbm] + ([("cgf", f)] if (ti == 0 and ab == 1 and kc == 15) else []))
                    si = (f % 2) * 2 + ab
                    act(sAB.ap[:, si, 0:N], PS[pbm][:, 0:N], AF.Sigmoid, PSR[pbm], sAB.cres(si, NT * 4))
                sa_i = (f % 2) * 2
                tt("dve", tAB.ap[:, sa_i, 0:N], PS[base][:, 0:N], sAB.ap[:, sa_i, 0:N], ALU.mult,
                   PSR[base] + sAB.cres(sa_i, NT * 4), tAB.cres(sa_i, NT * 4))
                tt("dve", tAB.ap[:, sa_i + 1, 0:N], PS[base + 1][:, 0:N], sAB.ap[:, sa_i + 1, 0:N], ALU.mult,
                   PSR[base + 1] + sAB.cres(sa_i + 1, NT * 4), tAB.cres(sa_i + 1, NT * 4))
                tt("pool", merged.ap[:, f, 0:N], tAB.ap[:, sa_i, 0:N], tAB.ap[:, sa_i + 1, 0:N], ALU.add,
                   tAB.cres(sa_i, NT * 4) + tAB.cres(sa_i + 1, NT * 4), merged.cres(f, NT * 2))

            S.tag = f"t{ti}-wout"
            nxt = ti + 1 if ti + 1 <= NPT else None
            nNS = (1 if nxt == NPT else 4) if nxt is not None else 0
            if nxt is not None:
                in_a_front(nxt, 0)
                if nNS > 1:
                    in_a_front(nxt, 1)
            def wout_load(n_):
                dma("sp", wout.ap[:, n_ % 2, :], wbf_out[n_], [("wbfout", n_ // 2)], wout.cres(n_ % 2, 8192))

            wout_load(0)
            wout_load(1)
            for n in range(8):
                slot = n % 2
                wo = wout.ap[:, slot, :].rearrange("p (k n) -> p k n", k=16)
                for m in range(NS):
                    pb = next_ps(6)
                    for kc in range(16):
                        mm(PS[pb][:, 0:256], merged.ap[:, kc, m * 128:(m + 1) * 128], wo[:, kc, :], kc == 0, kc == 15,
                           merged.cres(kc, NT * 2) + wout.cres(slot, 8192), PSR[pb])
                    xs_ = xbuf.ap[:, m, n * 256:(n + 1) * 256]
                    rs = rescnt[0] % 4
                    rescnt[0] += 1
                    tt("dve", tAB.ap[:, rs, 0:256], PS[pb][:, 0:256], modG.ap[:, n * 256:(n + 1) * 256], ALU.mult,
                       PSR[pb] + modG.res(), tAB.cres(rs, NT * 4))
                    tt("pool", xs_, xs_, tAB.ap[:, rs, 0:256], ALU.add, xbuf.cres(m, 8192) + tAB.cres(rs, NT * 4),
                       xbuf.cres(m, 8192))
                if n + 2 < 8:
                    wout_load(n + 2)
                if nxt is not None and n % 2 == 1:
                    mb = n // 2
                    if mb < nNS:
                        in_a_back(nxt, mb)
                        if mb + 2 < nNS:
                            in_a_front(nxt, mb + 2)
            for m in range(NS):
                act(ybuf.ap[:, m % 2, :], xbuf.ap[:, m, :], AF.Square, xbuf.cres(m, 8192),
                    ybuf.cres(m % 2, 8192) + statF.res(), accum_out=statF.ap[:, m:m + 1])
                ts("dve", statF.ap[:, 4 + m:5 + m], statF.ap[:, m:m + 1], 1.0 / D, RMS_EPS, ALU.mult, ALU.add,
                   statF.res(), statF.res())
                act(statF.ap[:, 8 + m:9 + m], statF.ap[:, 4 + m:5 + m], AF.Sqrt, statF.res(), statF.res())
                recip(statF.ap[:, 12 + m:13 + m], statF.ap[:, 8 + m:9 + m], statF.res(), statF.res())
                stt("dve", ybuf.ap[:, m % 2, :], xbuf.ap[:, m, :], statF.ap[:, 12 + m:13 + m], modF.ap, ALU.mult, ALU.mult,
                    xbuf.cres(m, 8192) + statF.res() + modF.res(), ybuf.cres(m % 2, 8192))
                dma("pool", ydst[m * 128:(m + 1) * 128, :], ybuf.ap[:, m % 2, :], ybuf.cres(m % 2, 8192),
                    [("yout", ti, m)], is_out=True)

            ckpt(f"out{ti}")
        memset("pool", U_ap(0)[:, :, 0:30], 0.0, U_res(0))

        for m in range(4):
            in_a_back(0, m)
            if m + 2 < 4:
                in_a_front(0, m + 2)
        for ti in range(NPT):
            process_tile(ti)

        for s in range(4):
            dma("sp", ckst.ap[:, :, 0:64], ck_d[s].rearrange("k (g d) -> k g d", g=4), (), ckst.res())
            dma("sp", ckst.ap[:, :, 64:128], ck_d[s].rearrange("k (g d) -> k g d", g=4), (), ckst.res())
            for g in range(4):
                tr(PS[0][:, g * 128:(g + 1) * 128], ckst.ap[:, g, :], identf.ap, ckst.res() + identf.res(), PSR[0])
            cp("dve", kTc.ap[:, :, s, :], PS[0][:, :].rearrange("p (g k) -> p g k", g=4), PSR[0], kTc.res())
            dma("pool", Vx.ap[:, s, :, 0:64], cv_d[s].rearrange("k (g d) -> k g d", g=4), (), Vx.cres(s, 1024))
            dma("pool", Vx.ap[:, s, :, 64:128], cv_d[s].rearrange("k (g d) -> k g d", g=4), (), Vx.cres(s, 1024))
        dma("sp", scst.ap[0:120, :], sc_d, (), scst.res())
        for c in range(8):
            pb = c % 2
            tr(PS[pb][:, 0:120], scst.ap[0:120, c * 128:(c + 1) * 128], identf.ap[0:120, 0:120],
               scst.res() + identf.res(), PSR[pb])
            cp("dve" if c % 2 == 0 else "act", Us.ap[:, c, :, 0:30],
               PS[pb][:, 0:120].rearrange("p (s r) -> p s r", s=4), PSR[pb], Us.res())
        process_tile(NPT)

    try:
        _construct()
    except _StopBuild:
        pass

    fin = S.add("sp", lambda e: e.nop(), (), ())
    fin.deps = list(S.out_dmas)

    from contextlib import ExitStack
    with ExitStack() as st:
        sems = {k: st.enter_context(nc.semaphore("sem_" + k)) for k in ("pe", "act", "dve", "pool", "sp")}
        dma_sems = [st.enter_context(nc.semaphore(f"dsem{i}")) for i in range(N_DMA_SEMS)]
        S.finalize(nc, sems, dma_sems)
        block = st.enter_context(nc.Block())

        @block.sync
        def _(e):
            S.emit_engine("sp", e)

        @block.tensor
        def _(e):
            S.emit_engine("pe", e)

        @block.scalar
        def _(e):
            S.emit_engine("act", e)

        @block.vector
        def _(e):
            S.emit_engine("dve", e)

        @block.gpsimd
        def _(e):
            S.emit_engine("pool", e)
    return nc


_NC_CACHE = {}


def _rope_tables(pos):
    half = 32
    inv = (10000.0 ** (-2.0 * np.arange(half, dtype=np.float32) / 64.0)).astype(np.float32)
    ang = pos.astype(np.float32)[None, :] * inv[:, None]
    cos = np.cos(ang).astype(np.float32)
    sin = np.sin(ang).astype(np.float32)
    return np.ascontiguousarray(np.tile(cos, (4, 1))), np.ascontiguousarray(np.tile(sin, (4, 1)))


def prep_inputs(x_prompt, x_sample, c_prompt, c_sample, cache_k, cache_v, state_conv,
                norm_g, w_ada, b_ada, w_in, sinks, w_dw, b_dw, ln_g, ln_b,
                w_proj_a, w_proj_b, w_out, final_g):
    f32 = np.float32
    A = lambda a: np.ascontiguousarray(np.asarray(a, dtype=f32))
    x_prompt, x_sample = A(x_prompt), A(x_sample)
    w_in0 = A(w_in)[0]
    cols = []
    for c in range(8):
        cols.append(np.arange(2560 + 128 * c, 2560 + 128 * c + 128))
        cols.append(np.arange(3584 + 128 * c, 3584 + 128 * c + 128))
    for g in range(4):
        k = np.arange(1024 + 64 * g, 1024 + 64 * g + 64)
        cols.append(np.concatenate([k, k]))
    for c in range(8):
        cols.append(np.arange(128 * c, 128 * c + 128))
    for c in range(8):
        cols.append(np.arange(1536 + 128 * c, 1536 + 128 * c + 128))
    for c in range(8):
        cols.append(np.arange(4608 + 128 * c, 4608 + 128 * c + 128))
    for f in range(16):
        cols.append(np.arange(5632 + 128 * f, 5632 + 128 * f + 128))
        cols.append(np.arange(7680 + 128 * f, 7680 + 128 * f + 128))
    colidx = np.concatenate(cols)
    assert colidx.shape[0] == NCH_IN * 128
    W = w_in0[:, colidx].reshape(16, 128, NCH_IN, 128)
    w_in_t = np.ascontiguousarray(W.transpose(2, 1, 0, 3)).reshape(NCH_IN, 128, 2048)
    w_v_t = np.ascontiguousarray(w_in0[:, 1280:1536].reshape(16, 128, 256).transpose(1, 0, 2)).reshape(128, 4096)

    def proj_t(w):
        w = A(w)[0].reshape(8, 128, 16, 128)
        return np.ascontiguousarray(w.transpose(2, 1, 0, 3)).reshape(16, 128, 1024)

    w_pa_t, w_pb_t = proj_t(w_proj_a), proj_t(w_proj_b)
    w_out_t = np.ascontiguousarray(A(w_out)[0].reshape(16, 128, 8, 256).transpose(2, 1, 0, 3)).reshape(8, 128, 4096)
    w_ada_t = np.ascontiguousarray(A(w_ada)[0].reshape(16, 128, 12, 512).transpose(2, 1, 0, 3))
    b_ada_bc = np.ascontiguousarray(np.broadcast_to(A(b_ada)[0][None, :], (128, 6144)))
    g_bc = np.ascontiguousarray(np.broadcast_to(A(norm_g)[0][None, :], (128, D)))
    fg_bc = np.ascontiguousarray(np.broadcast_to(A(final_g)[None, :], (128, D)))
    cos_p, sin_p = _rope_tables(np.arange(SEQ))
    cos_s1, sin_s1 = _rope_tables(1024 + np.arange(32))
    cos_s = np.ascontiguousarray(np.tile(cos_s1, (1, 4)))
    sin_s = np.ascontiguousarray(np.tile(sin_s1, (1, 4)))
    rmat = np.zeros((128, 128), f32)
    for m in range(128):
        if (m % 64) < 32:
            rmat[m + 32, m] = -1.0
        else:
            rmat[m - 32, m] = 1.0
    bf = ml_dtypes.bfloat16
    sk = A(sinks)[0]
    sinkrow = np.concatenate([np.repeat(sk, 64), np.repeat(sk, 32)])[None, :].astype(f32)
    wdw_t = np.ascontiguousarray(A(w_dw)[0].reshape(31, 8, 128).transpose(2, 1, 0)).reshape(128, 248)
    vt = lambda v: A(v)[0].reshape(8, 128).T
    vec_t = np.ascontiguousarray(np.concatenate([vt(b_dw), vt(ln_g), vt(ln_b)], axis=1))
    selc = np.zeros((128, 8), f32)
    selc[:, 0:4] = 1.0 / 128.0
    for s_ in range(4):
        selc[32 * s_:32 * s_ + 32, 4 + s_] = 1.0 / 32.0
    common = dict(
        w_ada_t=w_ada_t, b_ada_bc=b_ada_bc, g_bc=g_bc, fg_bc=fg_bc, w_in_t=w_in_t, w_v_t=w_v_t,
        w_pa_t=w_pa_t, w_pb_t=w_pb_t, w_out_t=w_out_t, cos_p=cos_p, sin_p=sin_p, cos_s=cos_s, sin_s=sin_s,
        rmat=rmat.astype(bf), identb=np.eye(128, dtype=f32).astype(bf), onesb=np.ones((128, 128), f32).astype(bf),
        identf=np.eye(128, dtype=f32), onesf=np.full((128, 128), 1.0 / 1024.0, f32), sinkrow=sinkrow,
        wdw_t=wdw_t, vec_t=vec_t, selc=selc,
    )
    c_prompt, c_sample = A(c_prompt), A(c_sample)
    ck, cv, scv = A(cache_k)[0], A(cache_v)[0], A(state_conv)[0]
    in_maps = []
    for c in range(8):
        cp_ = c_prompt[c].reshape(16, 128).T
        ctp = np.ascontiguousarray(np.broadcast_to(cp_[:, :, None], (128, 16, 128)))
        cs_ = c_sample[4 * c:4 * c + 4].reshape(4, 16, 128)
        cts = np.ascontiguousarray(np.repeat(cs_.transpose(2, 1, 0), 32, axis=2))
        m = dict(common)
        m.update(
            xp=x_prompt[c], xs=np.ascontiguousarray(x_sample[4 * c:4 * c + 4].reshape(128, D)),
            ctp=ctp, cts=cts,
            cache_k=np.ascontiguousarray(ck[4 * c:4 * c + 4].reshape(4, 128, 256)),
            cache_v=np.ascontiguousarray(cv[4 * c:4 * c + 4].reshape(4, 128, 256)),
            state_conv=np.ascontiguousarray(scv[4 * c:4 * c + 4].reshape(120, 1024)),
        )
        in_maps.append(m)
    return in_maps


def kernel(**inputs):
    f32 = np.float32
    in_maps = prep_inputs(**inputs)
    if "nc" not in _NC_CACHE:
        _NC_CACHE["nc"] = build_nc()
    nc = _NC_CACHE["nc"]
    res = run_bass_kernel_spmd(nc, in_maps, core_ids=list(range(8)))
    R = res.results
    y_prompt = np.stack([R[c]["yp"] for c in range(8)]).astype(f32)
    y_sample = np.concatenate([R[c]["ys"].reshape(4, 32, D) for c in range(8)]).astype(f32)
    nkp = np.stack([R[c]["nkp"].reshape(128, 4, 64) for c in range(8)])[None].astype(f32)
    nvp = np.stack([R[c]["nvp"].reshape(128, 4, 64) for c in range(8)])[None].astype(f32)
    ncp = np.stack([R[c]["ncp"] for c in range(8)])[None].astype(f32)
    nks = np.concatenate([R[c]["nks"].reshape(4, 128, 4, 64) for c in range(8)])[None].astype(f32)
    nvs = np.concatenate([R[c]["nvs"].reshape(4, 128, 4, 64) for c in range(8)])[None].astype(f32)
    ncs = np.concatenate([R[c]["ncs"] for c in range(8)])[None].astype(f32)
    return (y_prompt, y_sample, nkp, nvp, ncp, nks, nvs, ncs)
```
